# Optimizing a Trainium2 kernel written in Bass

```python
import math
import jax, jax.numpy as jnp
from jax import lax
import numpy as np

D_MODEL = 1024
BATCH = 32
SEQ = 256
DEPTH = 2
DEC_BATCH = 2
DEC_SEQ = 4096
PAST_LEN = 256

GRID_W = 64
HEAD_DIM = 64
GQA_WIDTH = D_MODEL // 2
DIFF_WIDTH = D_MODEL // 4
HGRN_WIDTH = D_MODEL // 4
GQA_Q_HEADS = GQA_WIDTH // HEAD_DIM
GQA_KV_HEADS = 2
GQA_GROUP = GQA_Q_HEADS // GQA_KV_HEADS
DIFF_HEADS = DIFF_WIDTH // HEAD_DIM
DIFF_QK_DIM = HEAD_DIM // 2
HGRN_HEADS = HGRN_WIDTH // HEAD_DIM
HGRN_KEY_DIM = HEAD_DIM
HGRN_VAL_DIM = HEAD_DIM
MIX_WIDTH = GQA_WIDTH + DIFF_WIDTH + HGRN_WIDTH
SPLIT_SIZES = (GQA_WIDTH, GQA_KV_HEADS * HEAD_DIM, GQA_KV_HEADS * HEAD_DIM, GQA_WIDTH,
               DIFF_WIDTH, DIFF_WIDTH, DIFF_WIDTH, DIFF_WIDTH,
               HGRN_WIDTH, HGRN_WIDTH, HGRN_WIDTH, HGRN_WIDTH, HGRN_WIDTH)
IN_WIDTH = sum(SPLIT_SIZES)
SPLIT_POINTS = tuple(sum(SPLIT_SIZES[:i + 1]) for i in range(len(SPLIT_SIZES) - 1))
Q_BLOCK = 128
SCAN_CHUNK = 64
ROPE_THETA = 10000.0
RMS_EPS = 1e-6
LN_EPS = 1e-5
FORGET_MIN = 1e-6
DEEPNORM_ALPHA = (2 * DEPTH) ** 0.25
DEEPNORM_BETA = (8 * DEPTH) ** -0.25

kernel_name = "hybrid_gqa_diffattn_hgrn2_diffusion_step"


def rms_norm(x, gain):
    xf = x.astype(jnp.float32)
    y = xf * lax.rsqrt(jnp.mean(xf * xf, axis=-1, keepdims=True) + RMS_EPS)
    return (y * gain.astype(jnp.float32)).astype(x.dtype)


def layer_norm(x, g, b):
    xf = x.astype(jnp.float32)
    mu = jnp.mean(xf, axis=-1, keepdims=True)
    xc = xf - mu
    var = jnp.mean(xc * xc, axis=-1, keepdims=True)
    return (xc * lax.rsqrt(var + LN_EPS) * g.astype(jnp.float32) + b.astype(jnp.float32)).astype(x.dtype)


def axial_rope_tables(n_tokens, dim):
    rows = n_tokens // GRID_W
    row = jnp.repeat(jnp.arange(rows, dtype=jnp.float32), GRID_W)
    col = jnp.tile(jnp.arange(GRID_W, dtype=jnp.float32), rows)
    quarter = dim // 4
    inv_freq = ROPE_THETA ** (-jnp.arange(quarter, dtype=jnp.float32) / quarter)
    ar = row[:, None] * inv_freq[None, :]
    ac = col[:, None] * inv_freq[None, :]
    ang = jnp.concatenate([ar, ar, ac, ac], axis=-1)
    return jnp.cos(ang), jnp.sin(ang)


def _rotate_half(u):
    u1, u2 = jnp.split(u, 2, axis=-1)
    return jnp.concatenate([-u2, u1], axis=-1)


def apply_axial_rope(x, cos, sin):
    shape = (1, cos.shape[0]) + (1,) * (x.ndim - 3) + (cos.shape[1],)
    c = cos.reshape(shape).astype(x.dtype)
    s = sin.reshape(shape).astype(x.dtype)
    xa, xb = jnp.split(x, 2, axis=-1)
    xr = jnp.concatenate([_rotate_half(xa), _rotate_half(xb)], axis=-1)
    return x * c + xr * s


def _query_blocks(q):
    b, n = q.shape[:2]
    q = q.reshape((b, n // Q_BLOCK, Q_BLOCK) + q.shape[2:])
    return jnp.moveaxis(q, 1, 0)


def _merge_blocks(o):
    o = jnp.moveaxis(o, 0, 1)
    return o.reshape((o.shape[0], o.shape[1] * o.shape[2]) + o.shape[3:])


def gqa_attention(q, k, v):
    b, n = q.shape[:2]
    qg = q.reshape(b, n, GQA_KV_HEADS, GQA_GROUP, HEAD_DIM)
    scale = HEAD_DIM ** -0.5

    def block(qb):
        s = jnp.einsum('bqgrd,bkgd->bgrqk', qb, k).astype(jnp.float32) * scale
        p = jax.nn.softmax(s, axis=-1).astype(v.dtype)
        return jnp.einsum('bgrqk,bkgd->bqgrd', p, v)

    o = _merge_blocks(lax.map(block, _query_blocks(qg)))
    return o.reshape(b, n, GQA_WIDTH)


def diff_attention(q, k, v, lam):
    scale = DIFF_QK_DIM ** -0.5

    def block(qb):
        s = jnp.einsum('bqhmd,bkhmd->bhmqk', qb, k).astype(jnp.float32) * scale
        p = jax.nn.softmax(s, axis=-1)
        a = p[:, :, 0] - lam * p[:, :, 1]
        return jnp.einsum('bhqk,bkhd->bqhd', a.astype(v.dtype), v)

    return _merge_blocks(lax.map(block, _query_blocks(q)))


def hgrn_scan(q, k, g, v, s0):
    b, n, h, _ = q.shape
    nc = n // SCAN_CHUNK

    def chunks(a):
        a = a.reshape(b, nc, SCAN_CHUNK, h, a.shape[-1])
        return jnp.transpose(a, (1, 0, 3, 2, 4))

    causal = jnp.tril(jnp.ones((SCAN_CHUNK, SCAN_CHUNK), dtype=bool))[:, :, None]

    def step(S, xs):
        qc, kc, gc, vc = xs
        G = jnp.cumsum(gc, axis=2)
        inter = jnp.einsum('bhck,bhkv->bhcv', qc * jnp.exp(G), S)
        diff = G[:, :, :, None, :] - G[:, :, None, :, :]
        decay = jnp.where(causal, jnp.exp(jnp.where(causal, diff, 0.0)), 0.0)
        A = jnp.einsum('bhtk,bhsk,bhtsk->bhts', qc, kc, decay)
        intra = jnp.einsum('bhts,bhsv->bhtv', A, vc)
        G_last = G[:, :, -1:, :]
        S_new = (jnp.exp(G_last[:, :, 0, :])[..., None] * S
                 + jnp.einsum('bhsk,bhsv->bhkv', kc * jnp.exp(G_last - G), vc))
        return S_new, inter + intra

    S_fin, o = lax.scan(step, s0, (chunks(q), chunks(k), chunks(g), chunks(v)))
    o = jnp.transpose(o, (1, 0, 3, 2, 4)).reshape(b, n, h, v.shape[-1])
    return o, S_fin


def hgrn_gates(z, lb):
    b, n, _ = z.shape
    z = z.reshape(b, n, HGRN_HEADS, HGRN_KEY_DIM).astype(jnp.float32)
    lb = lb.reshape(HGRN_HEADS, HGRN_KEY_DIM)
    k = (1.0 - lb) * jax.nn.sigmoid(-z)
    f = lb + (1.0 - lb) * jax.nn.sigmoid(z)
    g = jnp.log(jnp.maximum(f, FORGET_MIN))
    return k, g


def modulation(cond, w_ada_l, b_ada_l):
    m = jnp.einsum('bd,de->be', jax.nn.silu(cond), w_ada_l) + b_ada_l
    shift, scale, gate = jnp.split(m, 3, axis=-1)
    return shift[:, None, :], scale[:, None, :], gate[:, None, :]


def mixer(h, ctx, rope, lam_init, lb, w_in, q_norm, k_norm, lam_p, subln, hg_norm, w_out):
    b, n, _ = h.shape
    parts = jnp.split(jnp.einsum('bnd,de->bne', h, w_in), SPLIT_POINTS, axis=-1)
    a_q, a_k, a_v, a_g, d_q, d_k, d_v, d_g, r_q, r_ff, r_fb, r_i, r_g = parts

    a_q = rms_norm(a_q.reshape(b, n, GQA_Q_HEADS, HEAD_DIM), q_norm)
    a_k = rms_norm(a_k.reshape(b, n, GQA_KV_HEADS, HEAD_DIM), k_norm)
    a_v = a_v.reshape(b, n, GQA_KV_HEADS, HEAD_DIM)
    d_q = d_q.reshape(b, n, DIFF_HEADS, 2, DIFF_QK_DIM)
    d_k = d_k.reshape(b, n, DIFF_HEADS, 2, DIFF_QK_DIM)
    d_v = d_v.reshape(b, n, DIFF_HEADS, HEAD_DIM)

    if ctx is None:
        ka, va, kd, vd = a_k, a_v, d_k, d_v
        s0_f = jnp.zeros((b, HGRN_HEADS, HGRN_KEY_DIM, HGRN_VAL_DIM), jnp.float32)
        s0_b = s0_f
    else:
        cg_k, cg_v, cd_k, cd_v, st = ctx
        (cos_a, sin_a), (cos_d, sin_d) = rope
        a_q = apply_axial_rope(a_q, cos_a, sin_a)
        a_k = apply_axial_rope(a_k, cos_a, sin_a)
        d_q = apply_axial_rope(d_q, cos_d, sin_d)
        d_k = apply_axial_rope(d_k, cos_d, sin_d)
        ka = jnp.concatenate([cg_k.astype(a_k.dtype), a_k], axis=1)
        va = jnp.concatenate([cg_v.astype(a_v.dtype), a_v], axis=1)
        cd_k = cd_k.reshape(cd_k.shape[:2] + (DIFF_HEADS, 2, DIFF_QK_DIM))
        kd = jnp.concatenate([cd_k.astype(d_k.dtype), d_k], axis=1)
        vd = jnp.concatenate([cd_v.astype(d_v.dtype), d_v], axis=1)
        s0_f = st[:, 0].astype(jnp.float32)
        s0_b = st[:, 1].astype(jnp.float32)

    out_a = gqa_attention(a_q, ka, va) * jax.nn.silu(a_g)

    lp = lam_p.astype(jnp.float32)
    lam = jnp.exp(jnp.sum(lp[0] * lp[1])) - jnp.exp(jnp.sum(lp[2] * lp[3])) + lam_init
    o_d = diff_attention(d_q, kd, vd, lam)
    o_d = rms_norm(o_d, subln) * (1.0 - lam_init)
    out_d = o_d.reshape(b, n, DIFF_WIDTH) * jax.nn.silu(d_g)

    q_r = jax.nn.silu(r_q).reshape(b, n, HGRN_HEADS, HGRN_KEY_DIM).astype(jnp.float32)
    v_r = r_i.reshape(b, n, HGRN_HEADS, HGRN_VAL_DIM).astype(jnp.float32)
    k_f, g_f = hgrn_gates(r_ff, lb[0])
    k_b, g_b = hgrn_gates(r_fb, lb[1])
    o_f, sf = hgrn_scan(q_r, k_f, g_f, v_r, s0_f)
    flip = lambda a: jnp.flip(a, axis=1)
    o_b, sb = hgrn_scan(flip(q_r), flip(k_b), flip(g_b), flip(v_r), s0_b)
    o_r = rms_norm(o_f + flip(o_b), hg_norm).astype(h.dtype)
    out_r = o_r.reshape(b, n, HGRN_WIDTH) * jax.nn.silu(r_g)

    y = jnp.einsum('bne,ed->bnd', jnp.concatenate([out_a, out_d, out_r], axis=-1), w_out)
    if ctx is None:
        ctx_out = (a_k, a_v, d_k.reshape(b, n, DIFF_HEADS, HEAD_DIM), d_v,
                   jnp.stack([sf, sb], axis=1).astype(h.dtype))
        return y, ctx_out
    return y, None


def setup_inputs(seed: int = 0) -> dict:
    key = jax.random.key(seed)
    ks = jax.random.split(key, 24)
    f32 = jnp.float32
    nrm = lambda k, shape: jax.random.normal(k, shape, f32)
    s_in = D_MODEL ** -0.5
    return {
        "x_prompt": nrm(ks[0], (BATCH, SEQ, D_MODEL)),
        "x_sample": nrm(ks[1], (DEC_BATCH, DEC_SEQ, D_MODEL)),
        "cache_gqa_k": nrm(ks[2], (DEC_BATCH, DEPTH, PAST_LEN, GQA_KV_HEADS, HEAD_DIM)),
        "cache_gqa_v": nrm(ks[3], (DEC_BATCH, DEPTH, PAST_LEN, GQA_KV_HEADS, HEAD_DIM)),
        "cache_diff_k": nrm(ks[4], (DEC_BATCH, DEPTH, PAST_LEN, DIFF_HEADS, HEAD_DIM)),
        "cache_diff_v": nrm(ks[5], (DEC_BATCH, DEPTH, PAST_LEN, DIFF_HEADS, HEAD_DIM)),
        "state_hgrn": 0.5 * nrm(ks[6], (DEC_BATCH, DEPTH, 2, HGRN_HEADS, HGRN_KEY_DIM, HGRN_VAL_DIM)),
        "c": nrm(ks[7], (DEC_BATCH, D_MODEL)),
        "c_ctx": nrm(ks[8], (D_MODEL,)),
        "w_ada": 0.5 * s_in * nrm(ks[9], (DEPTH, D_MODEL, 3 * D_MODEL)),
        "b_ada": 0.02 * nrm(ks[10], (DEPTH, 3 * D_MODEL)),
        "w_in": s_in * nrm(ks[11], (DEPTH, D_MODEL, IN_WIDTH)),
        "gqa_q_norm": 1.0 + 0.02 * nrm(ks[12], (DEPTH, HEAD_DIM)),
        "gqa_k_norm": 1.0 + 0.02 * nrm(ks[13], (DEPTH, HEAD_DIM)),
        "diff_lambda": 0.1 * nrm(ks[14], (DEPTH, 4, DIFF_QK_DIM)),
        "diff_subln": 1.0 + 0.02 * nrm(ks[15], (DEPTH, HEAD_DIM)),
        "hgrn_lower_bounds": 0.1 * nrm(ks[16], (DEPTH, 2, HGRN_WIDTH)),
        "hgrn_norm": 1.0 + 0.02 * nrm(ks[17], (DEPTH, HEAD_DIM)),
        "w_out": DEEPNORM_BETA * MIX_WIDTH ** -0.5 * nrm(ks[18], (DEPTH, MIX_WIDTH, D_MODEL)),
        "ln_g": 1.0 + 0.02 * nrm(ks[19], (DEPTH, D_MODEL)),
        "ln_b": 0.02 * nrm(ks[20], (DEPTH, D_MODEL)),
    }


def reference(x_prompt, x_sample, cache_gqa_k, cache_gqa_v, cache_diff_k, cache_diff_v, state_hgrn,
              c, c_ctx, w_ada, b_ada, w_in, gqa_q_norm, gqa_k_norm, diff_lambda, diff_subln,
              hgrn_lower_bounds, hgrn_norm, w_out, ln_g, ln_b):
    lbs = jax.nn.softmax(hgrn_lower_bounds.astype(jnp.float32), axis=0)
    lbs = jnp.cumsum(lbs, axis=0) - lbs[0:1]

    x = x_prompt
    gk_l, gv_l, dk_l, dv_l, st_l = [], [], [], [], []
    for l in range(DEPTH):
        lam_init = 0.8 - 0.6 * math.exp(-0.3 * l)
        shift, scale, gate = modulation(c_ctx[None, :], w_ada[l], b_ada[l])
        h = x * (1.0 + scale) + shift
        y, (gk, gv, dk, dv, st) = mixer(h, None, None, lam_init, lbs[l], w_in[l], gqa_q_norm[l],
                                        gqa_k_norm[l], diff_lambda[l], diff_subln[l], hgrn_norm[l], w_out[l])
        x = layer_norm(DEEPNORM_ALPHA * x + gate * y, ln_g[l], ln_b[l])
        gk_l.append(gk); gv_l.append(gv); dk_l.append(dk); dv_l.append(dv); st_l.append(st)
    y_prompt = x
    new_gqa_k = jnp.stack(gk_l, axis=1)
    new_gqa_v = jnp.stack(gv_l, axis=1)
    new_diff_k = jnp.stack(dk_l, axis=1)
    new_diff_v = jnp.stack(dv_l, axis=1)
    new_state_hgrn = jnp.stack(st_l, axis=1)

    n_lat = x_sample.shape[1]
    rope = (axial_rope_tables(n_lat, HEAD_DIM), axial_rope_tables(n_lat, DIFF_QK_DIM))
    x = x_sample
    for l in range(DEPTH):
        lam_init = 0.8 - 0.6 * math.exp(-0.3 * l)
        shift, scale, gate = modulation(c, w_ada[l], b_ada[l])
        h = x * (1.0 + scale) + shift
        ctx = (cache_gqa_k[:, l], cache_gqa_v[:, l], cache_diff_k[:, l], cache_diff_v[:, l], state_hgrn[:, l])
        y, _ = mixer(h, ctx, rope, lam_init, lbs[l], w_in[l], gqa_q_norm[l], gqa_k_norm[l],
                     diff_lambda[l], diff_subln[l], hgrn_norm[l], w_out[l])
        x = layer_norm(DEEPNORM_ALPHA * x + gate * y, ln_g[l], ln_b[l])
    y_sample = x
    return (y_prompt, y_sample, new_gqa_k, new_gqa_v, new_diff_k, new_diff_v, new_state_hgrn)
```

```python
import math
import os
import numpy as np
from contextlib import ExitStack
import concourse.bass as bass
import concourse.mybir as mybir
from concourse.bass_utils import run_bass_kernel_spmd

F32 = mybir.dt.float32
BF16 = mybir.dt.bfloat16
AF = mybir.ActivationFunctionType
ALU = mybir.AluOpType
AX = mybir.AxisListType

L = 2
NT = 8
NKT_S = 34
ALPHA = (2 * L) ** 0.25
RMS_EPS = 1e-6
LN_EPS = 1e-5

WSEG = [(0, 768, 0), (768, 1280, 1536), (1280, 1536, 1024), (1536, 1792, 768),
        (1792, 2048, 1280), (2048, 2304, 2048), (2304, 2560, 2560),
        (2560, 3072, 3072), (3072, 3328, 2816), (3328, 3584, 2304)]


class Tok:
    def __init__(self, name):
        self.name = name
        self.w = {}
        self.r = {}
        self.dsem = None
        self.dcnt = 0
        self.alias = []

    def events(self):
        return list(self.w.items()) + list(self.r.items())


class Eng:
    def __init__(self, kb, eng, sem, self_sync=True):
        self.kb = kb
        self.eng = eng
        self.sem = sem
        self.cnt = 0
        self.seen = {}
        self.self_sync = self_sync

    def _wait(self, evs):
        need = {}
        for sem, val in evs:
            if sem is self.sem and not self.self_sync:
                continue
            if self.seen.get(sem, 0) >= val:
                continue
            if need.get(sem, 0) < val:
                need[sem] = val
        for sem, val in need.items():
            self.eng.wait_ge(sem, val)
            self.seen[sem] = val

    def _deps(self, r, w, after):
        evs = []
        for b in r:
            evs += list(b.w.items())
        for b in w:
            evs += b.events()
            for a in b.alias:
                evs += a.events()
        for b in after:
            evs += b.events()
        return evs

    def op(self, fn, r=(), w=(), after=()):
        self._wait(self._deps(r, w, after))
        ins = fn()
        self.cnt += 1
        ins.then_inc(self.sem, 1)
        for b in r:
            b.r[self.sem] = self.cnt
        for b in w:
            b.r = {}
            b.w[self.sem] = self.cnt
        return ins

    def barrier(self):
        if self.cnt > 0 and self.seen.get(self.sem, 0) < self.cnt:
            self.eng.wait_ge(self.sem, self.cnt)
            self.seen[self.sem] = self.cnt

    def dma(self, out, in_, r=(), w=(), after=(), owner=None, **kw):
        self._wait(self._deps(r, w, after))
        own = owner or (w[0] if w else r[0])
        if own.dsem is None:
            own.dsem = self.kb.new_sem("d_" + own.name)
        ins = self.eng.dma_start(out=out, in_=in_, **kw)
        own.dcnt += 16
        ins.then_inc(own.dsem, 16)
        for b in r:
            b.r[own.dsem] = own.dcnt
        for b in w:
            b.r = {}
            b.w[own.dsem] = own.dcnt
        return ins


class KB:
    def __init__(self, nc, es):
        self.nc = nc
        self.es = es
        self.nsem = 0
        self.pe = Eng(self, nc.tensor, self.new_sem("pe"), self_sync=False)
        self.act = Eng(self, nc.scalar, self.new_sem("act"))
        self.dve = Eng(self, nc.vector, self.new_sem("dve"))
        self.pool = Eng(self, nc.gpsimd, self.new_sem("pool"))
        self.sp = Eng(self, nc.sync, self.new_sem("sp"))

    def new_sem(self, name):
        self.nsem += 1
        return self.es.enter_context(self.nc.semaphore("s%d_%s" % (self.nsem, name)))

    def sb(self, name, shape, dt):
        return self.es.enter_context(self.nc.sbuf_tensor("sb_" + name, list(shape), dt))

    def ps(self, name, shape, dt):
        return self.es.enter_context(self.nc.psum_tensor("pp_" + name, list(shape), dt))


class Rot:
    def __init__(self, items):
        self.items = items
        self.i = 0

    def __call__(self):
        it = self.items[self.i % len(self.items)]
        self.i += 1
        return it


def build(debug=None):
    nc = bass.Bass("TRN2", target_bir_lowering=False)
    es = ExitStack()
    kb = KB(nc, es)
    PE, ACT, DVE, POOL, SP = kb.pe, kb.act, kb.dve, kb.pool, kb.sp
    P, A, V, G = PE.op, ACT.op, DVE.op, POOL.op
    te, se, ve, ge = nc.tensor, nc.scalar, nc.vector, nc.gpsimd

    def din(name, shape, dt=F32):
        return nc.dram_tensor(name, list(shape), dt, kind="ExternalInput").ap()

    def dout(name, shape):
        return nc.dram_tensor(name, list(shape), F32, kind="ExternalOutput").ap()

    x_in = [din("xp", [1024, 1024]), din("xs", [1024, 1024])]
    condT_d = din("condT", [2, 128, 8])
    w_ada = din("w_ada", [L, 1024, 3072])
    badaT_d = din("badaT", [L, 128, 16])
    badag_d = din("badag", [L, 128, 1024])
    w_in = din("w_in", [L, 1024, 3584])
    w_out = din("w_out", [L, 1024, 1024])
    gains_d = din("gains", [L, 128, 256])
    dlam_d = din("dlam", [L, 128, 128])
    hlb_d = din("hlb", [128, 1024])
    lnbc_d = din("lnbc", [L, 128, 2048])
    cgk = din("cgk", [L, 256, 128])
    cgv = din("cgv", [L, 256, 128])
    cdk = din("cdk", [L, 256, 256])
    cdv = din("cdv", [L, 256, 256])
    st0 = din("st0", [L, 2, 4, 64, 64])
    rope_d = din("rope", [1024, 192])
    cf_d = din("cf", [128, 656])
    ident_d = din("ident", [128, 128])
    y_out = [dout("yp", [1024, 1024]), dout("ys", [1024, 1024])]
    ngk = dout("ngk", [4, L, 256, 128])
    ngv = dout("ngv", [4, L, 256, 128])
    ndk = dout("ndk", [4, L, 256, 256])
    ndv = dout("ndv", [4, L, 256, 256])
    nst = dout("nst", [4, L, 2, 4, 64, 64])
    ginA = [nc.dram_tensor("ginA%d" % i, [384, 1024], BF16).ap() for i in range(4)]
    ginB = [nc.dram_tensor("ginB%d" % i, [384, 1024], BF16).ap() for i in range(4)]
    goutA = [nc.dram_tensor("goutA%d" % i, [4 * 384, 1024], BF16).ap() for i in range(L)]
    goutB = [nc.dram_tensor("goutB%d" % i, [4 * 384, 1024], BF16).ap() for i in range(L)]
    gin2 = [nc.dram_tensor("ginh%d" % i, [128, 260], F32).ap() for i in range(L)]
    gout2 = [nc.dram_tensor("gouth%d" % i, [512, 260], F32).ap() for i in range(L)]
    winb = [nc.dram_tensor("winb%d" % i, [1024, 3584], BF16).ap() for i in range(L)]
    woutb = [nc.dram_tensor("woutb%d" % i, [1024, 1024], BF16).ap() for i in range(L)]
    wadab = [nc.dram_tensor("wadab%d" % i, [1024, 3072], BF16).ap() for i in range(L)]
    T_winb = [Tok("winb%d" % i) for i in range(L)]
    T_woutb = [Tok("woutb%d" % i) for i in range(L)]
    T_wadab = [Tok("wadab%d" % i) for i in range(L)]
    T_gin = [Tok("gin%d" % i) for i in range(4)]
    T_gout = [Tok("gout%d" % i) for i in range(L)]
    T_gin2 = [Tok("ginh%d" % i) for i in range(L)]
    T_gout2 = [Tok("gouth%d" % i) for i in range(L)]
    dbg_outs = {}

    REG = kb.sb("REG", [128, 28672], BF16)
    win_v = REG[:, :].rearrange("p (c n) -> p c n", c=8)
    kT = REG[:, 0:4352]
    dkT = REG[:, 4352:13056].rearrange("p (h n) -> p h n", h=2)
    VA4 = REG[:, 13056:21896].rearrange("p (k h c) -> p k h c", h=4, c=65)
    lnbc = REG[:, 23944:28040].bitcast(F32)
    wout_v = REG[:, 0:8192].rearrange("p (c n) -> p c n", c=8)
    PTb = [REG[:, 21896 + i * 1024:21896 + (i + 1) * 1024] for i in range(2)]
    ckb = REG[:, 21896:22152].rearrange("p (j c) -> p j c", j=2)
    cdb = REG[:, 22152:22664].rearrange("p (j c) -> p j c", j=2)
    W_IN, LNBC, W_OUT = (Tok(n) for n in ("W_IN", "LNBC", "W_OUT"))
    KTr = [Tok("KT%d" % i) for i in range(5)]
    DKTr = [Tok("DKT%d" % i) for i in range(5)]
    VATr = [[Tok("VAT%d_%d" % (hg, i)) for i in range(5)] for hg in range(2)]
    ALLV = VATr[0] + VATr[1]
    W_IN.alias = KTr + DKTr + ALLV + [LNBC, W_OUT]
    W_OUT.alias = [W_IN] + KTr + DKTr
    for tk_ in KTr + DKTr:
        tk_.alias = [W_IN, W_OUT]
    for tk_ in ALLV:
        tk_.alias = [W_IN]
    LNBC.alias = [W_IN]

    xres = kb.sb("xres", [128, NT, 1024], F32)
    T_x = [Tok("x%d" % t) for t in range(NT)]
    modbc = kb.sb("modbc", [128, 1024], F32)
    modT = kb.sb("modT", [128, 16], F32)
    T_modT = Tok("modT")
    T_mod = Tok("mod")
    concat = kb.sb("concat", [128, NT, 1024], BF16)
    T_cc = [Tok("cc%d" % t) for t in range(NT)]
    qT = kb.sb("qT", [128, 4, 1024], BF16)
    T_qT = [Tok("qT%d" % t) for t in range(NT)]
    qdT = kb.sb("qdT", [128, 2, 1024], BF16)
    T_qdT = [Tok("qdT%d" % t) for t in range(NT)]
    Sloc = kb.sb("Sloc", [128, 2, 16, 2, 64], BF16)
    T_Sloc = [Tok("Sloc%d" % t) for t in range(NT)]
    oacc = kb.sb("oacc", [128, NT, 256], F32)
    T_oacc = [Tok("oacc%d" % t) for t in range(NT)]
    qeT = kb.sb("qeT", [128, 2, NT, 2, 128], BF16)
    T_qeT = [Tok("qeT%d" % t) for t in range(NT)]
    DTt = kb.sb("DTt", [128, 2, 16, 2], F32)
    EMt = kb.sb("EMt", [128, 2, 16, 2], F32)
    GTt = kb.sb("GTt", [128, 2, 16, 2], F32)
    T_stat = [Tok("stat%d" % t) for t in range(NT)]
    cf = kb.sb("cf", [128, 656], F32)
    T_cf = Tok("cf")
    ident = kb.sb("ident", [128, 128], BF16)
    T_id = Tok("ident")
    gn = kb.sb("gn", [128, 256], F32)
    T_gn = Tok("gn")
    lbt = kb.sb("lbt", [128, 512], F32)
    oml = kb.sb("oml", [128, 512], F32)
    T_lb = Tok("lb")
    lamt = kb.sb("lamt", [128, 8], F32)
    T_lam = Tok("lam")
    T_Pb = [Tok("PTb%d" % i) for i in range(2)]
    for tk_ in T_Pb:
        tk_.alias = [W_IN]
    W_IN.alias += T_Pb
    rotP = Rot([(PTb[i], T_Pb[i]) for i in range(2)])
    SCR = kb.sb("SCR", [128, 2560], F32)
    s1024 = [SCR[:, 0:1024], SCR[:, 1024:2048]]
    T_s1024 = [Tok("s1024_%d" % i) for i in range(2)]
    rot1024 = Rot([(s1024[i], T_s1024[i]) for i in range(2)])
    g_d = [SCR[:, 0:512], SCR[:, 512:1024]]
    T_g = [Tok("g%d" % i) for i in range(2)]
    kk_d = [SCR[:, 1024:1536], SCR[:, 1536:2048]]
    T_kkd = [Tok("kk%d" % i) for i in range(2)]
    cT_d = SCR[:, 2048:2560].bitcast(BF16).rearrange("p (c n) -> p c n", c=8)
    T_cTd = Tok("cT")
    qe_d = SCR[:, 2048:2304].bitcast(BF16)
    ke_d = SCR[:, 2304:2560].bitcast(BF16)
    T_qed, T_ked = Tok("qe"), Tok("ke")
    T_s1024[0].alias = list(T_g)
    T_s1024[1].alias = list(T_kkd)
    for tk_ in T_g:
        tk_.alias = [T_s1024[0]]
    for tk_ in T_kkd:
        tk_.alias = [T_s1024[1]]
    T_cTd.alias = [T_qed, T_ked]
    T_qed.alias = [T_cTd]
    T_ked.alias = [T_cTd]
    keT_d = kb.sb("keT", [128, 512], BF16)
    T_keTd = Tok("keT")
    atm_d = [kb.sb("atm%d" % i, [128, 512], BF16) for i in range(2)]
    T_atmd = [Tok("atm%d" % i) for i in range(2)]
    qf2_d = [kb.sb("qf2_%d" % i, [128, 512], BF16) for i in range(2)]
    T_qf2d = [Tok("qf2_%d" % i) for i in range(2)]
    dqf_d = [kb.sb("dqf%d" % i, [128, 256], BF16) for i in range(2)]
    T_dqfd = [Tok("dqf%d" % i) for i in range(2)]
    vb_d = [kb.sb("vb%d" % i, [128, 256], BF16) for i in range(2)]
    T_vbd = [Tok("vb%d" % i) for i in range(2)]
    sgq_d = [kb.sb("sgq%d" % i, [128, 256], F32) for i in range(2)]
    T_sgqd = [Tok("sgq%d" % i) for i in range(2)]
    el_d = kb.sb("el", [128, 16], F32)
    mu_all = kb.sb("mu_all", [128, NT, 4], F32)
    T_mua = [Tok("mu%d" % t) for t in range(NT)]
    T_eld = Tok("el")
    s512 = [kb.sb("s512_%d" % i, [128, 512], F32) for i in range(3)]
    rot512 = Rot([(s512[i], Tok("s512_%d" % i)) for i in range(3)])
    b512 = [kb.sb("b512_%d" % i, [128, 512], BF16) for i in range(2)]
    rotb512 = Rot([(b512[i], Tok("b512_%d" % i)) for i in range(2)])
    s16 = [kb.sb("s16_%d" % i, [128, 16], F32) for i in range(6)]
    rot16 = Rot([(s16[i], Tok("s16_%d" % i)) for i in range(6)])
    hTb = [kb.sb("hT%d" % i, [128, 8, 128], BF16) for i in range(2)]
    rothT = Rot([(hTb[i], Tok("hT%d" % i)) for i in range(2)])
    kvo = [kb.sb("kvo%d" % i, [128, 768], F32) for i in range(1)]
    rotkvo = Rot([(kvo[i], Tok("kvo%d" % i)) for i in range(1)])
    kvbb = [kb.sb("kvb%d" % i, [128, 768], BF16) for i in range(2)]
    rotkvb = Rot([(kvbb[i], Tok("kvb%d" % i)) for i in range(2)])
    ktst = [kb.sb("ktst%d" % i, [128, 3, 128], BF16) for i in range(1)]
    rotktst = Rot([(ktst[i], Tok("ktst%d" % i)) for i in range(1)])
    ropeb = [kb.sb("rope%d" % i, [128, 192], F32) for i in range(1)]
    rotrope = Rot([(ropeb[i], Tok("rope%d" % i)) for i in range(1)])
    Srun = kb.sb("Srun", [128, 2, 2, 64], F32)
    T_Srun = [Tok("Srun0"), Tok("Srun1")]
    Spb = [kb.sb("Sp%d" % i, [128, 2, 64], BF16) for i in range(4)]
    rotSp_d = [Rot([(Spb[2 * d_ + i], Tok("Sp%d" % (2 * d_ + i))) for i in range(2)]) for d_ in range(2)]
    stmp = [kb.sb("stmp%d" % i, [128, 2, 64], F32) for i in range(2)]
    rotstmp_d = [Rot([(stmp[d_], Tok("stmp%d" % d_))]) for d_ in range(2)]
    hx = kb.sb("hx", [128, 2, 130], F32)
    T_hx = Tok("hx")
    hg = kb.sb("hg", [128, 4, 260], F32)
    T_hg = Tok("hg")
    cand = kb.sb("cand", [128, 2, 2, 64], F32)
    T_cand = Tok("cand")
    s0t = kb.sb("s0t", [128, 2, 2, 64], F32)
    T_s0 = Tok("s0t")
    condT = kb.sb("condT", [128, 8], F32)
    T_cond = Tok("condT")
    sc8 = kb.sb("sc8", [128, 8], F32)
    T_sc8 = Tok("sc8")
    screp = kb.sb("screp", [128, 8, 128], BF16)
    T_screp = Tok("screp")
    badaT = kb.sb("badaT", [128, 16], F32)
    T_bada = Tok("bada")
    T_ckb = Tok("ckb")
    T_cdb = Tok("cdb")
    T_ckb.alias = [W_IN, T_Pb[0]]
    T_cdb.alias = [W_IN, T_Pb[0]]
    W_IN.alias += [T_ckb, T_cdb]
    T_Pb[0].alias += [T_ckb, T_cdb]
    rotcT = Rot([(cT_d, T_cTd)])
    nstst = [kb.sb("nstst%d" % i, [128, 2, 64], F32) for i in range(1)]
    rotnst = Rot([(nstst[i], Tok("nstst%d" % i)) for i in range(1)])
    qblk = [kb.sb("qblk%d" % i, [128, 512], BF16) for i in range(1)]
    T_qblk = [Tok("qblk%d" % i) for i in range(1)]
    rotqblk = Rot([(qblk[i], T_qblk[i]) for i in range(1)])
    qgb = [kb.sb("qgb%d" % i, [128, 512], BF16) for i in range(2)]
    T_qgb = [Tok("qgb%d" % i) for i in range(2)]
    print("sbuf bytes remaining:", nc.sbuf_bytes_remaining)

    PSP = [kb.ps("psp%d" % i, [128, 1024], F32) for i in range(2)]
    PS2 = kb.ps("ps2", [128, 1024], F32)
    PB = [PSP[0][:, 0:512], PSP[0][:, 512:1024], PSP[1][:, 0:512], PSP[1][:, 512:1024], PS2[:, 0:512], PS2[:, 512:1024]]
    T_PB = [Tok("PB%d" % i) for i in range(6)]
    PTT = kb.ps("ptt", [128, 2048], BF16)
    PT = [PTT[:, 0:1024], PTT[:, 1024:2048]]
    T_PT = [Tok("PT%d" % i) for i in range(2)]
    PSP3 = PTT[:, :].bitcast(F32)
    PB += [PSP3[:, 0:512], PSP3[:, 512:1024]]
    T_PB += T_PT
    SPAIR = [(PSP[0][:, :], 0, 1), (PSP[1][:, :], 2, 3), (PSP3, 6, 7)]
    rotPB = Rot(list(range(6)))
    rotPT = Rot(list(range(2)))
    rotS = Rot([0, 1, 2])
    rotO = Rot([4, 5])

    def dbg(name, ap, toks, shape):
        if debug is None or name not in debug:
            return
        d = dout("dbg_" + name, shape)
        tk = Tok("dbg_" + name)
        POOL.dma(out=d, in_=ap, r=list(toks), w=[tk])
        dbg_outs[name] = tk

    SP.dma(out=cf[:, :], in_=cf_d, w=[T_cf])
    for i in range(1):
        G(lambda: ge.memset(qblk[i][:, :], 0.0), w=[T_qblk[i]])
    for i in range(2):
        G(lambda: ge.memset(qgb[i][:, :], 0.0), w=[T_qgb[i]])
    POOL.dma(out=ident[:, :], in_=ident_d, w=[T_id])
    LcT = [cf[:, 0:128], cf[:, 128:256]]
    MSK = [cf[:, 256:384], cf[:, 384:512]]
    IND = [cf[:, 512:516], cf[:, 516:520]]
    SEL = cf[:, 520:524]
    IDF = cf[:, 528:656]

    out_toks = []

    def bview(ap, shape, axis):
        return ap.unsqueeze(axis).broadcast_to(list(shape))

    STAGE = int(os.environ.get("KDBG_STAGE", "9"))
    NPH = int(os.environ.get("KDBG_PHASES", "4"))
    SUB = int(os.environ.get("KDBG_SUB", "99"))
    NTL = int(os.environ.get("KDBG_TILES", "8"))

    conv_list = []
    for (dst, src, tk) in ((woutb[0], w_out[0], T_woutb[0]), (winb[1], w_in[1], T_winb[1]),
                           (wadab[1], w_ada[1], T_wadab[1]), (woutb[1], w_out[1], T_woutb[1]),
                           (winb[0], w_in[0], T_winb[0]), (wadab[0], w_ada[0], T_wadab[0])):
        for c in range(8):
            conv_list.append((dst[c * 128:(c + 1) * 128, :], src[c * 128:(c + 1) * 128, :], tk))

    def pump(n):
        for _ in range(n):
            if conv_list:
                dst, src, tk = conv_list.pop(0)
                POOL.dma(out=dst, in_=src, w=[tk])

    def phase(g, l, ph):
        lam_init = 0.8 - 0.6 * math.exp(-0.3 * l)
        nkt = NKT_S if g == 1 else 8

        SP.dma(out=gn[:, :], in_=gains_d[l], w=[T_gn])
        SP.dma(out=badaT[:, :], in_=badaT_d[l], w=[T_bada])
        SP.dma(out=condT[:, :], in_=condT_d[g], w=[T_cond])
        dl, T_dl = rot512()
        SP.dma(out=dl[:, 0:128], in_=dlam_d[l], w=[T_dl])
        dl4 = dl[:, 0:128].rearrange("p (a b x) -> p a b x", a=2, b=2)
        pr, T_pr = rot512()
        V(lambda: ve.tensor_tensor(out=pr[:, 0:64].rearrange("p (a x) -> p a x", a=2), in0=dl4[:, :, 0, :],
                                   in1=dl4[:, :, 1, :], op=ALU.mult), r=[T_dl], w=[T_pr])
        V(lambda: ve.tensor_reduce(out=lamt[:, 2:4], in_=pr[:, 0:64].rearrange("p (a x) -> p a x", a=2),
                                   axis=AX.X, op=ALU.add), r=[T_pr], w=[T_lam])
        A(lambda: se.activation(out=lamt[:, 4:6], in_=lamt[:, 2:4], func=AF.Exp), r=[T_lam], w=[T_lam])
        V(lambda: ve.tensor_tensor(out=lamt[:, 0:1], in0=lamt[:, 4:5], in1=lamt[:, 5:6], op=ALU.subtract),
          r=[T_lam], w=[T_lam])
        V(lambda: ve.tensor_scalar(out=lamt[:, 0:1], in0=lamt[:, 0:1], scalar1=lam_init, scalar2=None, op0=ALU.add),
          r=[T_lam], w=[T_lam])
        V(lambda: ve.tensor_scalar(out=lamt[:, 1:2], in0=lamt[:, 0:1], scalar1=-1.0, scalar2=None, op0=ALU.mult),
          r=[T_lam], w=[T_lam])
        V(lambda: ve.tensor_scalar(out=gn[:, 128:192], in0=gn[:, 128:192], scalar1=(1.0 - lam_init), scalar2=None,
                                   op0=ALU.mult), r=[T_gn], w=[T_gn])
        if l == 0:
            V(lambda: ve.memset(lbt[:, :], 0.0), w=[T_lb])
            V(lambda: ve.memset(oml[:, :], 1.0), w=[T_lb])
        else:
            hl, T_hl = rot1024()
            SP.dma(out=hl[:, :], in_=hlb_d, w=[T_hl])
            V(lambda: ve.tensor_tensor(out=lbt[:, :], in0=hl[:, 512:1024], in1=hl[:, 0:512], op=ALU.subtract),
              r=[T_hl], w=[T_lb])
            A(lambda: se.activation(out=lbt[:, :], in_=lbt[:, :], func=AF.Sigmoid), r=[T_lb], w=[T_lb])
            V(lambda: ve.tensor_scalar(out=oml[:, :], in0=lbt[:, :], scalar1=-1.0, scalar2=1.0, op0=ALU.mult,
                                       op1=ALU.add), r=[T_lb], w=[T_lb])

        if ph == 0:
            wsrc = w_in[l].rearrange("(c p) n -> p c n", p=128)
            for (a, b, pos) in WSEG:
                POOL.dma(out=win_v[:, :, pos:pos + (b - a)], in_=wsrc[:, :, a:b], w=[W_IN])
        else:
            wsrc = winb[l].rearrange("(c p) n -> p c n", p=128)
            for (a, b, pos) in WSEG:
                POOL.dma(out=win_v[:, :, pos:pos + (b - a)], in_=wsrc[:, :, a:b], r=[T_winb[l]], w=[W_IN])

        sg8, T_sg8 = rot16()
        A(lambda: se.activation(out=sg8[:, 0:8], in_=condT[:, :], func=AF.Sigmoid), r=[T_cond], w=[T_sg8])
        V(lambda: ve.tensor_tensor(out=sc8[:, :], in0=condT[:, :], in1=sg8[:, 0:8], op=ALU.mult),
          r=[T_cond, T_sg8], w=[T_sc8])
        V(lambda: ve.tensor_copy(out=screp[:, :, :], in_=bview(sc8[:, :], [128, 8, 128], 2)), r=[T_sc8], w=[T_screp])
        ccflat = concat[:, :, :].rearrange("p t n -> p (t n)")
        for c in range(8):
            slot = ccflat[:, (c % 2) * 3072:(c % 2) * 3072 + 3072]
            tk = T_cc[c % 2]
            if ph == 0:
                POOL.dma(out=slot, in_=w_ada[l, c * 128:(c + 1) * 128, :], w=[tk], after=T_cc)
            else:
                SP.dma(out=slot, in_=wadab[l][c * 128:(c + 1) * 128, :], r=[T_wadab[l]], w=[tk], after=T_cc)
            for j in range(6):
                P(lambda: te.matmul(PB[j], lhsT=screp[:, c, :], rhs=slot[:, j * 512:(j + 1) * 512],
                                    start=(c == 0), stop=(c == 7)), r=[T_screp, tk], w=[T_PB[j]])
        for j in range(4):
            dg, T_dg = rot512()
            V(lambda: ve.tensor_tensor(out=dg[:, :].rearrange("p (c n) -> p c n", c=4),
                                       in0=PB[j].rearrange("p (c n) -> p c n", c=4),
                                       in1=bview(IDF, [128, 4, 128], 1), op=ALU.mult), r=[T_PB[j], T_cf], w=[T_dg])
            V(lambda: ve.tensor_reduce(out=modT[:, j * 4:(j + 1) * 4], in_=dg[:, :].rearrange("p (c n) -> p c n", c=4),
                                       axis=AX.X, op=ALU.add), r=[T_dg], w=[T_modT])
        V(lambda: ve.tensor_tensor(out=modT[:, :], in0=modT[:, :], in1=badaT[:, :], op=ALU.add),
          r=[T_modT, T_bada], w=[T_modT])
        V(lambda: ve.tensor_scalar(out=modT[:, 8:16], in0=modT[:, 8:16], scalar1=1.0, scalar2=None, op0=ALU.add),
          r=[T_modT], w=[T_modT])
        bg, T_bg = rot1024()
        SP.dma(out=bg[:, :], in_=badag_d[l], w=[T_bg])
        for j in range(2):
            V(lambda: ve.tensor_tensor(out=modbc[:, j * 512:(j + 1) * 512], in0=PB[4 + j], in1=bg[:, j * 512:(j + 1) * 512],
                                       op=ALU.add), r=[T_PB[4 + j], T_bg], w=[T_mod])

        if STAGE < 1:
            return
        def rstd_of(sq_ap, T_sq, nh):
            ss, T_ss = rot16()
            V(lambda: ve.tensor_reduce(out=ss[:, 0:nh], in_=sq_ap.rearrange("p (h d) -> p h d", d=64), axis=AX.X,
                                       op=ALU.add), r=[T_sq], w=[T_ss])
            V(lambda: ve.tensor_scalar(out=ss[:, 0:nh], in0=ss[:, 0:nh], scalar1=1.0 / 64, scalar2=RMS_EPS,
                                       op0=ALU.mult, op1=ALU.add), r=[T_ss], w=[T_ss])
            A(lambda: se.activation(out=ss[:, 0:nh], in_=ss[:, 0:nh], func=AF.Ln), r=[T_ss], w=[T_ss])
            A(lambda: se.activation(out=ss[:, 0:nh], in_=ss[:, 0:nh], func=AF.Exp, scale=-0.5), r=[T_ss], w=[T_ss])
            return ss[:, 0:nh], T_ss

        def rope_apply(src, T_src, W, x, cos_ap, sin_ap, T_rope, out_ap, T_out):
            nb = W // (4 * x)
            rot, T_rot = rot512()
            s5 = src.rearrange("p (h j s x) -> p h j s x", j=2, s=2, x=x)
            r5 = rot[:, 0:W].rearrange("p (h j s x) -> p h j s x", j=2, s=2, x=x)
            sn = sin_ap.rearrange("p (j s x) -> p j s x", j=2, s=2)
            for s in range(2):
                V(lambda: ve.tensor_tensor(out=r5[:, :, :, s, :], in0=s5[:, :, :, 1 - s, :],
                                           in1=bview(sn[:, :, s, :], [128, nb, 2, x], 1), op=ALU.mult),
                  r=[T_src, T_rope], w=[T_rot])
            V(lambda: ve.tensor_tensor(out=src.rearrange("p (h c) -> p h c", c=4 * x),
                                       in0=src.rearrange("p (h c) -> p h c", c=4 * x),
                                       in1=bview(cos_ap, [128, nb, 4 * x], 1), op=ALU.mult),
              r=[T_src, T_rope], w=[T_src])
            V(lambda: ve.tensor_tensor(out=out_ap, in0=src, in1=rot[:, 0:W], op=ALU.add), r=[T_src, T_rot], w=[T_out])

        def rmsnorm_ps(src_ps, T_ps, nh, gain_ap, out_ap, T_out, rope=None):
            W = nh * 64
            sq, T_sq = rot512()
            raw, T_raw = rot512()
            A(lambda: se.activation(out=sq[:, 0:W], in_=src_ps, func=AF.Square), r=[T_ps], w=[T_sq])
            A(lambda: se.activation(out=raw[:, 0:W], in_=src_ps, func=AF.Copy), r=[T_ps], w=[T_raw])
            rs, T_rs = rstd_of(sq[:, 0:W], T_sq, nh)
            raw3 = raw[:, 0:W].rearrange("p (h d) -> p h d", d=64)
            V(lambda: ve.tensor_tensor(out=raw3, in0=raw3, in1=bview(rs, [128, nh, 64], 2), op=ALU.mult),
              r=[T_raw, T_rs], w=[T_raw])
            if rope is None:
                V(lambda: ve.tensor_tensor(out=out_ap.rearrange("p (h d) -> p h d", d=64), in0=raw3,
                                           in1=bview(gain_ap, [128, nh, 64], 1), op=ALU.mult),
                  r=[T_raw, T_gn], w=[T_out])
            else:
                V(lambda: ve.tensor_tensor(out=raw3, in0=raw3, in1=bview(gain_ap, [128, nh, 64], 1), op=ALU.mult),
                  r=[T_raw, T_gn], w=[T_raw])
                cos_ap, sin_ap, T_rope = rope
                rope_apply(raw[:, 0:W], T_raw, W, 16, cos_ap, sin_ap, T_rope, out_ap, T_out)

        def gate_block(pb, t, lo):
            sg, T_sg = rot512()
            A(lambda: se.activation(out=sg[:, :], in_=PB[pb], func=AF.Sigmoid), r=[T_PB[pb]], w=[T_sg])
            V(lambda: ve.tensor_tensor(out=concat[:, t, lo:lo + 512], in0=sg[:, :], in1=PB[pb], op=ALU.mult),
              r=[T_sg, T_PB[pb]], w=[T_cc[t]])

        def proj_gen(t):
            p_ = t % 2
            if l == 0:
                SP.dma(out=xres[:, t, :], in_=x_in[g][t * 128:(t + 1) * 128, :], w=[T_x[t]])
            if g == 1:
                rb, T_rb = rotrope()
                SP.dma(out=rb[:, :], in_=rope_d[t * 128:(t + 1) * 128, :], w=[T_rb])
                rpA = (rb[:, 0:64], rb[:, 64:128], T_rb)
                rpD = (rb[:, 128:160], rb[:, 160:192], T_rb)
            for c in range(8):
                P(lambda: te.transpose(out=PS2[:, c * 128:(c + 1) * 128], in_=xres[:, t, c * 128:(c + 1) * 128],
                                       identity=IDF), r=[T_x[t], T_cf], w=[T_PB[4], T_PB[5]])
            hT, T_hT = rothT()
            for c in range(8):
                V(lambda: ve.tensor_scalar(out=hT[:, c, :], in0=PS2[:, c * 128:(c + 1) * 128], scalar1=modT[:, 8 + c:9 + c],
                                           scalar2=modT[:, c:c + 1], op0=ALU.mult, op1=ALU.add),
                  r=[T_PB[4], T_PB[5], T_modT], w=[T_hT])

            def mm_block(blk):
                pb = rotPB()
                for c in range(8):
                    P(lambda: te.matmul(PB[pb], lhsT=hT[:, c, :], rhs=win_v[:, c, blk * 512:(blk + 1) * 512],
                                        start=(c == 0), stop=(c == 7)), r=[T_hT, W_IN], w=[T_PB[pb]])
                return pb

            pb = mm_block(0)
            qf, T_qf = rotb512()
            rmsnorm_ps(PB[pb], T_PB[pb], 8, gn[:, 0:64], qf[:, :], T_qf, rope=(rpA if g == 1 else None))
            qf2, T_qf2 = qf2_d[p_], T_qf2d[p_]
            G(lambda: ge.tensor_copy(out=qf2[:, :].rearrange("p (r g d) -> p r g d", r=4, g=2),
                                     in_=qf[:, :].rearrange("p (g r d) -> p r g d", g=2, r=4)), r=[T_qf], w=[T_qf2])
            yield
            ko, T_ko = rotkvo()
            pb1 = mm_block(1)
            rmsnorm_ps(PB[pb1][:, 0:128], T_PB[pb1], 2, gn[:, 64:128], ko[:, 0:128], T_ko,
                       rope=(rpA if g == 1 else None))
            A(lambda: se.activation(out=ko[:, 128:256], in_=PB[pb1][:, 128:256], func=AF.Copy),
              r=[T_PB[pb1]], w=[T_ko])
            if g == 0:
                A(lambda: se.activation(out=ko[:, 256:512], in_=PB[pb1][:, 256:512], func=AF.Copy),
                  r=[T_PB[pb1]], w=[T_ko])
            else:
                dkr, T_dkr = rot512()
                A(lambda: se.activation(out=dkr[:, 0:256], in_=PB[pb1][:, 256:512], func=AF.Copy),
                  r=[T_PB[pb1]], w=[T_dkr])
                rope_apply(dkr[:, 0:256], T_dkr, 256, 8, rpD[0], rpD[1], rpD[2], ko[:, 256:512], T_ko)
            pb2 = mm_block(2)
            A(lambda: se.activation(out=ko[:, 512:768], in_=PB[pb2][:, 256:512], func=AF.Copy),
              r=[T_PB[pb2]], w=[T_ko])
            dqf, T_dqf = dqf_d[p_], T_dqfd[p_]
            if g == 0:
                A(lambda: se.activation(out=dqf[:, 0:256], in_=PB[pb2][:, 0:256], func=AF.Copy),
                  r=[T_PB[pb2]], w=[T_dqf])
            else:
                dqr, T_dqr = rot512()
                A(lambda: se.activation(out=dqr[:, 0:256], in_=PB[pb2][:, 0:256], func=AF.Copy),
                  r=[T_PB[pb2]], w=[T_dqr])
                rope_apply(dqr[:, 0:256], T_dqr, 256, 8, rpD[0], rpD[1], rpD[2], dqf[:, 0:256], T_dqf)
            if g == 0:
                bb, hf = t // 2, t % 2
                tk = Tok("o_kv")
                out_toks.append(tk)
                for (dst, lo, hi) in ((ngk, 0, 128), (ngv, 128, 256), (ndk, 256, 512), (ndv, 512, 768)):
                    SP.dma(out=dst[bb, l, hf * 128:(hf + 1) * 128, :], in_=ko[:, lo:hi], r=[T_ko], w=[tk], owner=T_ko)
            kb_, T_kb = rotkvb()
            G(lambda: ge.tensor_copy(out=kb_[:, :], in_=ko[:, :]), r=[T_ko], w=[T_kb])
            yield
            pb = mm_block(3)
            gate_block(pb, t, 0)
            pb = mm_block(4)
            gate_block(pb, t, 512)
            yield
            pb5 = mm_block(5)
            sgq, T_sgq = sgq_d[p_], T_sgqd[p_]
            A(lambda: se.activation(out=sgq[:, 0:256], in_=PB[pb5][:, 0:256], func=AF.Sigmoid),
              r=[T_PB[pb5]], w=[T_sgq])
            V(lambda: ve.tensor_tensor(out=sgq[:, 0:256], in0=sgq[:, 0:256], in1=PB[pb5][:, 0:256], op=ALU.mult),
              r=[T_sgq, T_PB[pb5]], w=[T_sgq])
            vb, T_vb = vb_d[p_], T_vbd[p_]
            A(lambda: se.activation(out=vb[:, 0:256], in_=PB[pb5][:, 256:512], func=AF.Copy),
              r=[T_PB[pb5]], w=[T_vb])
            pb6 = mm_block(6)
            sig, T_sig = g_d[p_], T_g[p_]
            kk, T_kk = kk_d[p_], T_kkd[p_]
            A(lambda: se.activation(out=sig[:, :], in_=PB[pb6], func=AF.Sigmoid), r=[T_PB[pb6]], w=[T_sig])
            V(lambda: ve.tensor_tensor(out=sig[:, :], in0=sig[:, :], in1=oml[:, :], op=ALU.mult),
              r=[T_sig, T_lb], w=[T_sig])
            V(lambda: ve.tensor_tensor(out=kk[:, :], in0=oml[:, :], in1=sig[:, :], op=ALU.subtract),
              r=[T_sig, T_lb], w=[T_kk])
            V(lambda: ve.tensor_tensor(out=sig[:, :], in0=sig[:, :], in1=lbt[:, :], op=ALU.add),
              r=[T_sig, T_lb], w=[T_sig])
            V(lambda: ve.tensor_scalar(out=sig[:, :], in0=sig[:, :], scalar1=1e-6, scalar2=None, op0=ALU.max),
              r=[T_sig], w=[T_sig])
            A(lambda: se.activation(out=sig[:, :], in_=sig[:, :], func=AF.Ln), r=[T_sig], w=[T_sig])
            yield
            pt = rotPT()
            for r_ in range(4):
                P(lambda: te.transpose(out=PT[pt][:, r_ * 128:(r_ + 1) * 128], in_=qf2[:, r_ * 128:(r_ + 1) * 128],
                                       identity=ident[:, :]), r=[T_qf2, T_id], w=[T_PT[pt]])
            A(lambda: se.activation(out=qT[:, :, t * 128:(t + 1) * 128],
                                    in_=PT[pt][:, 0:512].rearrange("p (r n) -> p r n", r=4), func=AF.Copy),
              r=[T_PT[pt]], w=[T_qT[t]])
            pg = rotPB()
            for d in range(2):
                P(lambda: te.matmul(PB[pg][:, d * 256:(d + 1) * 256], lhsT=LcT[d], rhs=sig[:, d * 256:(d + 1) * 256],
                                    start=True, stop=True), r=[T_cf, T_sig], w=[T_PB[pg]])
            pst = rotPB()
            for d in range(2):
                for hp in range(2):
                    P(lambda: te.matmul(PB[pst][:, d * 8 + hp:d * 8 + 8:2], lhsT=sig[:, d * 256 + hp * 128:d * 256 + hp * 128 + 128],
                                        rhs=IND[d], start=True, stop=True), r=[T_cf, T_sig], w=[T_PB[pst]])
            pt = rotPT()
            srcs_ = [(kb_[:, 0:128], T_kb), (kb_[:, 256:384], T_kb), (kb_[:, 384:512], T_kb),
                     (dqf[:, 0:128], T_dqf), (dqf[:, 128:256], T_dqf)]
            for i, (sap, stk) in enumerate(srcs_):
                P(lambda: te.transpose(out=PT[pt][:, i * 128:(i + 1) * 128], in_=sap, identity=ident[:, :]),
                  r=[stk, T_id], w=[T_PT[pt]])
            e1, T_e1 = rot512()
            e2, T_e2 = rot512()
            A(lambda: se.activation(out=e1[:, :], in_=PB[pg], func=AF.Exp), r=[T_PB[pg]], w=[T_e1])
            A(lambda: se.activation(out=e2[:, :], in_=PB[pg], func=AF.Exp, scale=-1.0), r=[T_PB[pg]], w=[T_e2])
            qe, T_qe = qe_d, T_qed
            ke, T_ke = ke_d, T_ked
            V(lambda: ve.tensor_tensor(out=qe[:, :].rearrange("p (d n) -> p d n", d=2),
                                       in0=e1[:, :].rearrange("p (d n) -> p d n", d=2),
                                       in1=bview(sgq[:, 0:256], [128, 2, 256], 1), op=ALU.mult),
              r=[T_e1, T_sgq], w=[T_qe])
            V(lambda: ve.tensor_tensor(out=ke[:, :], in0=e2[:, :], in1=kk[:, :], op=ALU.mult),
              r=[T_e2, T_kk], w=[T_ke])
            ks, T_ks = rotktst()
            A(lambda: se.activation(out=ks[:, :, :], in_=PT[pt][:, 0:384].rearrange("p (a n) -> p a n", a=3),
                                    func=AF.Copy), r=[T_PT[pt]], w=[T_ks])
            A(lambda: se.activation(out=qdT[:, :, t * 128:(t + 1) * 128],
                                    in_=PT[pt][:, 384:640].rearrange("p (a n) -> p a n", a=2), func=AF.Copy),
              r=[T_PT[pt]], w=[T_qdT[t]])
            SP.dma(out=ginA[ph].rearrange("(a p) n -> p a n", p=128)[:, 0:3, t * 128:(t + 1) * 128], in_=ks[:, :, :],
                   r=[T_ks], w=[T_gin[ph]], owner=T_ks)
            SP.dma(out=ginB[ph][0:128, :].rearrange("r (q c) -> (r q) c", c=128)[t * 128:(t + 1) * 128, :],
                   in_=kb_[:, 128:256], r=[T_kb], w=[T_gin[ph]], owner=T_kb)
            SP.dma(out=ginB[ph][128:384, :].rearrange("r (q c) -> (r q) c", c=256)[t * 128:(t + 1) * 128, :],
                   in_=kb_[:, 512:768], r=[T_kb], w=[T_gin[ph]], owner=T_kb)
            st, T_st = rot16()
            A(lambda: se.activation(out=st[:, :], in_=PB[pst][:, 0:16], func=AF.Copy), r=[T_PB[pst]], w=[T_st])
            st4 = st[:, :].rearrange("p (d j c) -> p d j c", d=2, j=2)
            ex, T_ex = rot16()
            A(lambda: se.activation(out=ex[:, :], in_=st[:, :], func=AF.Exp), r=[T_st], w=[T_ex])
            ex4 = ex[:, :].rearrange("p (d j c) -> p d j c", d=2, j=2)
            el, T_el = el_d, T_eld
            V(lambda: ve.tensor_tensor(out=el[:, 0:8].rearrange("p (d c) -> p d c", d=2), in0=st4[:, :, 0, :],
                                       in1=st4[:, :, 1, :], op=ALU.subtract), r=[T_st], w=[T_el])
            A(lambda: se.activation(out=el[:, 0:8], in_=el[:, 0:8], func=AF.Exp), r=[T_el], w=[T_el])
            dst = lambda X: X[:, :, 2 * t:2 * t + 2, :].rearrange("p d c h -> p d (c h)")
            V(lambda: ve.tensor_copy(out=dst(DTt), in_=ex4[:, :, 0, :]), r=[T_ex], w=[T_stat[t]])
            V(lambda: ve.tensor_copy(out=dst(EMt), in_=ex4[:, :, 1, :]), r=[T_ex], w=[T_stat[t]])
            V(lambda: ve.tensor_copy(out=dst(GTt), in_=st4[:, :, 0, :]), r=[T_st], w=[T_stat[t]])
            yield
            pt = rotPT()
            for i in range(4):
                P(lambda: te.transpose(out=PT[pt][:, i * 128:(i + 1) * 128], in_=qe[:, i * 128:(i + 1) * 128],
                                       identity=ident[:, :]), r=[T_qe, T_id], w=[T_PT[pt]])
            for i in range(4):
                P(lambda: te.transpose(out=PT[pt][:, 512 + i * 128:512 + (i + 1) * 128],
                                       in_=ke[:, i * 128:(i + 1) * 128], identity=ident[:, :]),
                  r=[T_ke, T_id], w=[T_PT[pt]])
            A(lambda: se.activation(out=qeT[:, :, t, :, :], in_=PT[pt][:, 0:512].rearrange("p (d h n) -> p d h n", d=2, h=2),
                                    func=AF.Copy), r=[T_PT[pt]], w=[T_qeT[t]])
            keT, T_keT = keT_d, T_keTd
            A(lambda: se.activation(out=keT[:, :], in_=PT[pt][:, 512:1024], func=AF.Copy), r=[T_PT[pt]], w=[T_keT])
            keT4 = keT[:, :].rearrange("p (d h n) -> p d h n", d=2, h=2)
            yield
            atms = []
            for d in range(2):
                pa = rotPB()
                for h in (0, 2, 1, 3):
                    hp, hh = h // 2, h % 2
                    if h in (0, 1):
                        PE.barrier()
                    P(lambda: te.matmul(PB[pa][:, h * 128:(h + 1) * 128], lhsT=keT4[hh * 64:(hh + 1) * 64, d, hp, :],
                                        rhs=qeT[hh * 64:(hh + 1) * 64, d, t, hp, :], start=True, stop=True),
                      r=[T_keT, T_qeT[t]], w=[T_PB[pa]])
                atm, T_atm = atm_d[d], T_atmd[d]
                G(lambda: ge.memset(atm[:, :], 0.0), w=[T_atm])
                for h in range(4):
                    V(lambda: ve.copy_predicated(out=atm[:, h * 128:(h + 1) * 128], mask=MSK[d].bitcast(mybir.dt.uint32),
                                                 data=PB[pa][:, h * 128:(h + 1) * 128]),
                      r=[T_PB[pa], T_cf, T_atm], w=[T_atm])
                atms.append((atm, T_atm))
            yield
            pi = rotPB()
            PE.barrier()
            for h in range(4):
                for d in range(2):
                    atm, T_atm = atms[d]
                    P(lambda: te.matmul(PB[pi][:, h * 64:(h + 1) * 64], lhsT=atm[:, h * 128:(h + 1) * 128],
                                        rhs=vb[:, h * 64:(h + 1) * 64], start=(d == 0), stop=(d == 1),
                                        skip_group_check=True), r=[T_atm, T_vb], w=[T_PB[pi]])
            A(lambda: se.activation(out=oacc[:, t, :], in_=PB[pi][:, 0:256], func=AF.Copy), r=[T_PB[pi]], w=[T_oacc[t]])
            for cc in range(2):
                PE.barrier()
                for d in range(2):
                    for hp in range(2):
                        idx = (d * 2 + cc) * 2 + hp
                        P(lambda: te.matmul(PS2[:, idx * 128:(idx + 1) * 128],
                                            lhsT=ke[cc * 64:(cc + 1) * 64, d * 256 + hp * 128:d * 256 + hp * 128 + 128],
                                            rhs=vb[cc * 64:(cc + 1) * 64, hp * 128:(hp + 1) * 128], start=True, stop=True),
                          r=[T_ke, T_vb], w=[T_PB[4], T_PB[5]])
            PE.barrier()
            for hh in range(2):
                rows = slice(hh * 64, (hh + 1) * 64)
                V(lambda: ve.tensor_tensor(
                    out=Sloc[rows, :, 2 * t:2 * t + 2, :, :].rearrange("p d c h v -> p d (c h) v"),
                    in0=PS2[rows, :].rearrange("p (d c n) -> p d c n", d=2, c=4)[:, :, :, hh * 64:(hh + 1) * 64],
                    in1=bview(el[rows, 0:8].rearrange("p (d c) -> p d c", d=2), [64, 2, 4, 64], 3), op=ALU.mult),
                  r=[T_PB[4], T_PB[5], T_el], w=[T_Sloc[t]])

        gens = [proj_gen(t) for t in range(NTL)]
        for i in range(NTL + 2):
            for k_ in range(4):
                if i < NTL:
                    next(gens[i], None)
                if k_ >= 1 and 1 <= i <= NTL:
                    next(gens[i - 1], None)
                if k_ == 0 and 2 <= i <= NTL + 1:
                    next(gens[i - 2], None)
            pump(3)

        if STAGE < 2:
            return
        G(lambda: ge.memset(VA4[:, :, :, 64:65], 1.0), w=ALLV)
        if g == 1:
            POOL.dma(out=ckb[:, :, :], in_=cgk[l].rearrange("(j p) c -> p j c", p=128), w=[T_ckb])
            POOL.dma(out=cdb[:, :, :], in_=cdk[l].rearrange("(j p) c -> p j c", p=128), w=[T_cdb])
            pt = rotPT()
            for j in range(2):
                P(lambda: te.transpose(out=PT[pt][:, j * 128:(j + 1) * 128], in_=ckb[:, j, :], identity=ident[:, :]),
                  r=[T_ckb, T_id], w=[T_PT[pt]])
            for hp in range(2):
                for j in range(2):
                    P(lambda: te.transpose(out=PT[pt][:, 256 + (hp * 2 + j) * 128:256 + (hp * 2 + j + 1) * 128],
                                           in_=cdb[:, j, hp * 128:(hp + 1) * 128], identity=ident[:, :]),
                      r=[T_cdb, T_id], w=[T_PT[pt]])
            A(lambda: se.activation(out=kT[:, 0:256], in_=PT[pt][:, 0:256], func=AF.Copy), r=[T_PT[pt]], w=[KTr[0]])
            A(lambda: se.activation(out=dkT[:, :, 0:256], in_=PT[pt][:, 256:768].rearrange("p (h n) -> p h n", h=2),
                                    func=AF.Copy), r=[T_PT[pt]], w=[DKTr[0]])
            for gg in range(2):
                POOL.dma(out=VA4[:, 0:2, gg, 0:64], in_=cgv[l].rearrange("(j p) (g d) -> p j g d", p=128, g=2)[:, :, gg, :], w=[VATr[0][0]])
            ins = POOL
            ins._wait(ins._deps([T_gin[ph]], [T_gout[l]], []))
            T_gout[l].r = {}
            for (gi_, go_) in ((ginA[ph], goutA[l]), (ginB[ph], goutB[l])):
                ci = ge.collective_compute("AllGather", ALU.bypass, replica_groups=[[0, 1, 2, 3], [4, 5, 6, 7]],
                                           ins=[gi_.opt()], outs=[go_.opt()])
                csem = kb.new_sem("coll")
                ci.then_inc(csem, 1)
                T_gin[ph].r[csem] = 1
                T_gout[l].w[csem] = 1
            srcs = [(goutA[l][r_ * 384:(r_ + 1) * 384, :], goutB[l][r_ * 384:(r_ + 1) * 384, :], 256 + r_ * 1024,
                     2 + r_ * 8, T_gout[l]) for r_ in range(4)]
        else:
            srcs = [(ginA[ph], ginB[ph], 0, 0, T_gin[ph])]
        for ri, (src, srcB, koff, kt0, tk) in enumerate(srcs):
            SP.dma(out=kT[:, koff:koff + 1024], in_=src[0:128, :], r=[tk], w=[KTr[1 + ri]])
            vsrc = srcB[0:128, :].rearrange("r (q c) -> (r q) c", c=128).rearrange("(j p) (g d) -> p j g d", p=128, g=2)
            for gg in range(2):
                SP.dma(out=VA4[:, kt0:kt0 + 8, gg, 0:64], in_=vsrc[:, :, gg, :], r=[tk], w=[VATr[0][1 + ri]])
        for ri, (src, srcB, koff, kt0, tk) in enumerate(srcs):
            SP.dma(out=dkT[:, :, koff:koff + 1024], in_=src[128:384, :].rearrange("(h p) n -> p h n", p=128),
                   r=[tk], w=[DKTr[1 + ri]])

        def rg_of(kt):
            if g == 0:
                return 1
            return 0 if kt < 2 else 1 + (kt - 2) // 8

        if STAGE < 3:
            return
        def diff_v_loads(heads):
            if g == 1:
                for h in heads:
                    POOL.dma(out=VA4[:, 0:2, h, 0:64], in_=cdv[l].rearrange("(j p) (h d) -> p j h d", p=128, h=4)[:, :, h, :],
                             w=[VATr[h // 2][0]])
            for ri, (src, srcB, koff, kt0, tk) in enumerate(srcs):
                vsrc = srcB[128:384, :].rearrange("r (q c) -> (r q) c", c=256).rearrange("(j p) (h d) -> p j h d", p=128, h=4)
                for h in heads:
                    SP.dma(out=VA4[:, kt0:kt0 + 8, h, 0:64], in_=vsrc[:, :, h, :], r=[tk], w=[VATr[h // 2][1 + ri]])

        diff_v_loads((2, 3))

        def prefix_step(d, c, emit_inter):
            t, cc = c // 2, c % 2
            sr = Srun[:, d, :, :]
            if emit_inter:
                sp, T_sp = rotSp_d[d]()
                V(lambda: ve.tensor_tensor(out=sp[:, :, :], in0=sr, in1=bview(EMt[:, d, c, :], [128, 2, 64], 2),
                                           op=ALU.mult), r=[T_Srun[d], T_stat[t]], w=[T_sp])
            tm, T_tm = rotstmp_d[d]()
            V(lambda: ve.tensor_tensor(out=tm[:, :, :], in0=sr, in1=bview(DTt[:, d, c, :], [128, 2, 64], 2), op=ALU.mult),
              r=[T_Srun[d], T_stat[t]], w=[T_tm])
            V(lambda: ve.tensor_tensor(out=sr, in0=tm[:, :, :], in1=Sloc[:, d, c, :, :], op=ALU.add),
              r=[T_tm, T_Sloc[t]], w=[T_Srun[d]])
            if emit_inter:
                return sp, T_sp
            return None

        def run_dir(d, tiles, emit_inter=True):
            order = tiles if d == 0 else tiles[::-1]
            for t in order:
                chunks = [2 * t, 2 * t + 1] if d == 0 else [2 * t + 1, 2 * t]
                pi = None
                for c in chunks:
                    res = prefix_step(d, c, emit_inter)
                    if not emit_inter:
                        continue
                    sp, T_sp = res
                    cc = c % 2
                    if pi is None:
                        pi = rotPB()
                    for h in (0, 2, 1, 3):
                        hp, hh = h // 2, h % 2
                        if h in (0, 1):
                            PE.barrier()
                        P(lambda: te.matmul(PB[pi][cc * 64:(cc + 1) * 64, h * 64:(h + 1) * 64],
                                            lhsT=qeT[hh * 64:(hh + 1) * 64, d, t, hp, cc * 64:(cc + 1) * 64],
                                            rhs=sp[hh * 64:(hh + 1) * 64, hp, :], start=True, stop=True),
                          r=[T_qeT[t], T_sp], w=[T_PB[pi]])
                if emit_inter:
                    V(lambda: ve.tensor_tensor(out=oacc[:, t, :], in0=oacc[:, t, :], in1=PB[pi][:, 0:256], op=ALU.add),
                      r=[T_PB[pi], T_oacc[t]], w=[T_oacc[t]])
                yield

        def run_both(tiles):
            gs = [run_dir(0, tiles), run_dir(1, tiles)]
            live = True
            while live:
                live = False
                for g_ in gs:
                    try:
                        next(g_)
                        live = True
                    except StopIteration:
                        pass

        st_view = lambda ap: ap.rearrange("(hp hh) k v -> (hh k) hp v", hh=2)
        def sample_pass2_preamble():
            st_view = lambda ap: ap.rearrange("(hp hh) k v -> (hh k) hp v", hh=2)
            tiles = list(range(NT))
            for d in range(2):
                V(lambda: ve.memset(Srun[:, d, :, :], 0.0), w=[T_Srun[d]])
                for _ in run_dir(d, tiles, emit_inter=False):
                    pass
                V(lambda: ve.tensor_copy(out=hx[:, d, 0:128].rearrange("p (h v) -> p h v", h=2), in_=Srun[:, d, :, :]),
                  r=[T_Srun[d]], w=[T_hx])
                V(lambda: ve.tensor_reduce(out=hx[:, d, 128:130], in_=GTt[:, d, :, :].rearrange("p c h -> p h c"),
                                           axis=AX.X, op=ALU.add), r=T_stat, w=[T_hx])
            A(lambda: se.activation(out=hx[:, :, 128:130], in_=hx[:, :, 128:130], func=AF.Exp), r=[T_hx], w=[T_hx])
            SP.dma(out=gin2[l], in_=hx[:, :, :].rearrange("p d n -> p (d n)"), r=[T_hx], w=[T_gin2[l]], owner=T_hx)

        def sample_pass2_preamble_b():
            st_view = lambda ap: ap.rearrange("(hp hh) k v -> (hh k) hp v", hh=2)
            POOL._wait(POOL._deps([T_gin2[l]], [T_gout2[l]], []))
            ci = ge.collective_compute("AllGather", ALU.bypass, replica_groups=[[0, 1, 2, 3], [4, 5, 6, 7]],
                                       ins=[gin2[l].opt()], outs=[gout2[l].opt()])
            csem = kb.new_sem("collh")
            ci.then_inc(csem, 1)
            T_gin2[l].r[csem] = 1
            T_gout2[l].r = {}
            T_gout2[l].w[csem] = 1
            SP.dma(out=hg[:, :, :], in_=gout2[l].rearrange("(r p) n -> p r n", p=128), r=[T_gout2[l]], w=[T_hg])
            SP.dma(out=s0t[:, 0, :, :], in_=st_view(st0[l, 0]), w=[T_s0])
            SP.dma(out=s0t[:, 1, :, :], in_=st_view(st0[l, 1]), w=[T_s0])
            for d in range(2):
                R = cand[:, d, :, :]
                V(lambda: ve.tensor_copy(out=R, in_=s0t[:, d, :, :]), r=[T_s0], w=[T_cand])
                V(lambda: ve.memset(Srun[:, d, :, :], 0.0), w=[T_Srun[d]])
                ranks = [0, 1, 2, 3] if d == 0 else [3, 2, 1, 0]
                for p_ in ranks:
                    V(lambda: ve.scalar_tensor_tensor(out=Srun[:, d, :, :], in0=R, scalar=SEL[:, p_:p_ + 1],
                                                      in1=Srun[:, d, :, :], op0=ALU.mult, op1=ALU.add),
                      r=[T_cand, T_cf], w=[T_Srun[d]])
                    Sp_ = hg[:, p_, d * 130:d * 130 + 128].rearrange("p (h v) -> p h v", h=2)
                    Dp_ = hg[:, p_, d * 130 + 128:d * 130 + 130]
                    V(lambda: ve.tensor_tensor(out=R, in0=R, in1=bview(Dp_, [128, 2, 64], 2), op=ALU.mult),
                      r=[T_cand, T_hg], w=[T_cand])
                    V(lambda: ve.tensor_tensor(out=R, in0=R, in1=Sp_, op=ALU.add), r=[T_cand, T_hg], w=[T_cand])

        if g == 1:
            sample_pass2_preamble()

        def keytiles(t):
            if g == 1:
                return list(range(NKT_S))
            bb = t // 2
            return [2 * bb, 2 * bb + 1]

        def attn_unit(t, qk_fn, pv_fn, scale):
            kts = keytiles(t)
            npair = len(kts) // 2
            ob = rotO()
            sps = [rotS() for _ in range(npair)]

            def qk_pair(i):
                for j in range(2):
                    qk_fn(SPAIR[sps[i]][1 + j], kts[2 * i + j])

            for i in range(min(2, npair)):
                qk_pair(i)
            for i in range(npair):
                psp, b0, b1 = SPAIR[sps[i]]
                pbuf, T_pb = rotP()
                A(lambda: se.activation(out=pbuf, in_=psp, func=AF.Exp, scale=scale),
                  r=[T_PB[b0], T_PB[b1]], w=[T_pb])
                if i + 2 < npair:
                    qk_pair(i + 2)
                pv_fn(ob, pbuf, T_pb, (kts[2 * i], kts[2 * i + 1]), i == 0, i == npair - 1)
            osb, T_osb = rot512()
            A(lambda: se.activation(out=osb[0:65, :], in_=PB[ob][0:65, :], func=AF.Copy), r=[T_PB[ob]], w=[T_osb])
            for j in range(4):
                P(lambda: te.transpose(out=PB[ob][:, j * 65:(j + 1) * 65], in_=osb[0:65, j * 128:(j + 1) * 128],
                                       identity=IDF[0:65, 0:65]), r=[T_osb, T_cf], w=[T_PB[ob]])
            return ob

        for t in range(NT):
            for gg in range(2):
                rows = slice(gg * 64, (gg + 1) * 64)
                G(lambda: ge.tensor_copy(out=qgb[gg][rows, :].rearrange("p (r n) -> p r n", r=4),
                                         in_=qT[rows, :, t * 128:(t + 1) * 128]), r=[T_qT[t]], w=[T_qgb[gg]])

                def qk_fn(sbk, kt):
                    P(lambda: te.matmul(PB[sbk], lhsT=kT[:, kt * 128:(kt + 1) * 128], rhs=qgb[gg][:, :],
                                        start=True, stop=True), r=[KTr[rg_of(kt)], T_qgb[gg]], w=[T_PB[sbk]])

                def pv_fn(ob, pbuf, T_pb, ktp, first, last):
                    for j in range(2):
                        P(lambda: te.matmul(PB[ob][0:65, :], lhsT=VA4[:, ktp[j], gg, :], rhs=pbuf[:, j * 512:(j + 1) * 512],
                                            start=(first and j == 0), stop=(last and j == 1)),
                          r=[T_pb, VATr[0][rg_of(ktp[j])]], w=[T_PB[ob]])

                ob = attn_unit(t, qk_fn, pv_fn, 0.125)
                o3 = PB[ob][:, 0:260].rearrange("p (r c) -> p r c", c=65)
                rs, T_rs = rot16()
                V(lambda: ve.reciprocal(out=rs[:, 0:4], in_=o3[:, :, 64]), r=[T_PB[ob]], w=[T_rs])
                tmp, T_tmp = rot512()
                V(lambda: ve.tensor_tensor(out=tmp[:, 0:256].rearrange("p (r d) -> p r d", d=64), in0=o3[:, :, 0:64],
                                           in1=bview(rs[:, 0:4], [128, 4, 64], 2), op=ALU.mult),
                  r=[T_PB[ob], T_rs], w=[T_tmp])
                V(lambda: ve.tensor_tensor(out=concat[:, t, gg * 256:(gg + 1) * 256], in0=tmp[:, 0:256],
                                           in1=concat[:, t, gg * 256:(gg + 1) * 256], op=ALU.mult),
                  r=[T_tmp, T_cc[t]], w=[T_cc[t]])

        if STAGE < 4:
            return
        if g == 1:
            sample_pass2_preamble_b()
        diff_v_loads((0, 1))

        for hp in (1, 0):
            for t in range(NT):
                qb, T_qb = rotqblk()
                for j in range(4):
                    base = j * 32
                    G(lambda: ge.tensor_copy(out=qb[base:base + 32, j * 128:(j + 1) * 128],
                                             in_=qdT[base:base + 32, hp, t * 128:(t + 1) * 128]), r=[T_qdT[t]], w=[T_qb])

                def qk_fn(sbk, kt):
                    P(lambda: te.matmul(PB[sbk], lhsT=dkT[:, hp, kt * 128:(kt + 1) * 128], rhs=qb[:, :],
                                        start=True, stop=True), r=[DKTr[rg_of(kt)], T_qb], w=[T_PB[sbk]])

                def pv_fn(ob, pbuf, T_pb, ktp, first, last):
                    for j in range(2):
                        for hh in range(2):
                            P(lambda: te.matmul(PB[ob][0:65, hh * 256:(hh + 1) * 256], lhsT=VA4[:, ktp[j], hp * 2 + hh, :],
                                                rhs=pbuf[:, j * 512 + hh * 256:j * 512 + (hh + 1) * 256],
                                                start=(first and j == 0 and hh == 0), stop=(last and j == 1),
                                                skip_group_check=True), r=[T_pb, VATr[hp][rg_of(ktp[j])]], w=[T_PB[ob]])

                ob = attn_unit(t, qk_fn, pv_fn, 32 ** -0.5)
                o4 = PB[ob][:, 0:260].rearrange("p (h m c) -> p h m c", h=2, m=2)
                rs, T_rs = rot16()
                V(lambda: ve.reciprocal(out=rs[:, 0:4].rearrange("p (h m) -> p h m", m=2), in_=o4[:, :, :, 64]),
                  r=[T_PB[ob]], w=[T_rs])
                rs3 = rs[:, 0:4].rearrange("p (h m) -> p h m", m=2)
                V(lambda: ve.tensor_scalar(out=rs3[:, :, 1], in0=rs3[:, :, 1], scalar1=lamt[:, 1:2], scalar2=None,
                                           op0=ALU.mult), r=[T_rs, T_lam], w=[T_rs])
                t1, T_t1 = rot512()
                V(lambda: ve.tensor_tensor(out=t1[:, 0:256].rearrange("p (h m d) -> p h m d", h=2, m=2), in0=o4[:, :, :, 0:64],
                                           in1=bview(rs3, [128, 2, 2, 64], 3), op=ALU.mult),
                  r=[T_PB[ob], T_rs], w=[T_t1])
                t14 = t1[:, 0:256].rearrange("p (h m d) -> p h m d", h=2, m=2)
                od, T_od = rot512()
                od3 = od[:, 0:128].rearrange("p (h d) -> p h d", h=2)
                V(lambda: ve.tensor_tensor(out=od3, in0=t14[:, :, 0, :], in1=t14[:, :, 1, :], op=ALU.add),
                  r=[T_t1], w=[T_od])
                V(lambda: ve.tensor_tensor(out=t1[:, 256:384], in0=od[:, 0:128], in1=od[:, 0:128], op=ALU.mult),
                  r=[T_od], w=[T_t1])
                rsd, T_rsd = rstd_of(t1[:, 256:384], T_t1, 2)
                V(lambda: ve.tensor_tensor(out=od3, in0=od3, in1=bview(rsd, [128, 2, 64], 2), op=ALU.mult),
                  r=[T_od, T_rsd], w=[T_od])
                V(lambda: ve.tensor_tensor(out=od3, in0=od3, in1=bview(gn[:, 128:192], [128, 2, 64], 1), op=ALU.mult),
                  r=[T_od, T_gn], w=[T_od])
                lo = 512 + hp * 128
                V(lambda: ve.tensor_tensor(out=concat[:, t, lo:lo + 128], in0=od[:, 0:128], in1=concat[:, t, lo:lo + 128],
                                           op=ALU.mult), r=[T_od, T_cc[t]], w=[T_cc[t]])

        if STAGE < 5:
            return
        if g == 0:
            for bb in range(4):
                for d in range(2):
                    V(lambda: ve.memset(Srun[:, d, :, :], 0.0), w=[T_Srun[d]])
                run_both([2 * bb, 2 * bb + 1])
                for d in range(2):
                    ns, T_ns = rotnst()
                    V(lambda: ve.tensor_copy(out=ns[:, :, :], in_=Srun[:, d, :, :]), r=[T_Srun[d]], w=[T_ns])
                    tk = Tok("o_st")
                    out_toks.append(tk)
                    SP.dma(out=st_view(nst[bb, l, d]), in_=ns[:, :, :], r=[T_ns], w=[tk], owner=T_ns)
        else:
            tiles = list(range(NT))
            run_both(tiles)

        if STAGE < 6:
            return
        wsrc_o = woutb[l].rearrange("(c p) n -> p c n", p=128)
        SP.dma(out=wout_v[:, :, :], in_=wsrc_o, r=[T_woutb[l]], w=[W_OUT])
        SP.dma(out=lnbc, in_=lnbc_d[l], w=[LNBC])
        cT_bufs = [(cT_d, T_cTd), rothT.items[0]]

        def out_gen(t):
            xt = xres[:, t, :]
            mu = mu_all[:, t, :]
            T_mu = T_mua[t]
            sq, T_sq = rot512()
            V(lambda: ve.tensor_tensor(out=sq[:, 0:256], in0=oacc[:, t, :], in1=oacc[:, t, :], op=ALU.mult),
              r=[T_oacc[t]], w=[T_sq])
            rs, T_rs = rstd_of(sq[:, 0:256], T_sq, 4)
            yield
            o3 = oacc[:, t, :].rearrange("p (h d) -> p h d", d=64)
            V(lambda: ve.tensor_tensor(out=o3, in0=o3, in1=bview(rs, [128, 4, 64], 2), op=ALU.mult),
              r=[T_oacc[t], T_rs], w=[T_oacc[t]])
            V(lambda: ve.tensor_tensor(out=o3, in0=o3, in1=bview(gn[:, 192:256], [128, 4, 64], 1), op=ALU.mult),
              r=[T_oacc[t], T_gn], w=[T_oacc[t]])
            V(lambda: ve.tensor_tensor(out=concat[:, t, 768:1024], in0=oacc[:, t, :], in1=concat[:, t, 768:1024],
                                       op=ALU.mult), r=[T_oacc[t], T_cc[t]], w=[T_cc[t]])
            pt = rotPT()
            for c in range(8):
                P(lambda: te.transpose(out=PT[pt][:, c * 128:(c + 1) * 128], in_=concat[:, t, c * 128:(c + 1) * 128],
                                       identity=ident[:, :]), r=[T_cc[t], T_id], w=[T_PT[pt]])
            cT, T_cT = cT_bufs[t % 2]
            A(lambda: se.activation(out=cT[:, :, :].rearrange("p c n -> p (c n)"), in_=PT[pt][:, :], func=AF.Copy),
              r=[T_PT[pt]], w=[T_cT])
            yield
            tmp, T_tmp = rot1024()
            for j in range(2):
                pb = rotPB()
                for c in range(8):
                    P(lambda: te.matmul(PB[pb], lhsT=cT[:, c, :], rhs=wout_v[:, c, j * 512:(j + 1) * 512],
                                        start=(c == 0), stop=(c == 7)), r=[T_cT, W_OUT], w=[T_PB[pb]])
                V(lambda: ve.tensor_tensor(out=tmp[:, j * 512:(j + 1) * 512], in0=PB[pb],
                                           in1=modbc[:, j * 512:(j + 1) * 512], op=ALU.mult),
                  r=[T_PB[pb], T_mod], w=[T_tmp])
            V(lambda: ve.scalar_tensor_tensor(out=xt, in0=xt, scalar=ALPHA, in1=tmp[:, :], op0=ALU.mult,
                                              op1=ALU.add), r=[T_x[t], T_tmp], w=[T_x[t]])
            V(lambda: ve.tensor_reduce(out=mu[:, 0:1], in_=xt, axis=AX.X, op=ALU.add), r=[T_x[t]], w=[T_mu])
            V(lambda: ve.tensor_scalar(out=mu[:, 0:1], in0=mu[:, 0:1], scalar1=-1.0 / 1024, scalar2=None, op0=ALU.mult),
              r=[T_mu], w=[T_mu])
            yield
            V(lambda: ve.tensor_scalar(out=xt, in0=xt, scalar1=mu[:, 0:1], scalar2=None, op0=ALU.add),
              r=[T_x[t], T_mu], w=[T_x[t]])
            V(lambda: ve.memset(mu[:, 1:2], 0.0), r=[T_mu], w=[T_mu])
            junk = concat[:, t, :]
            A(lambda: se.activation(out=junk, in_=xt, func=AF.Square, accum_out=mu[:, 1:2]),
              r=[T_x[t], T_mu], w=[T_cc[t], T_mu])
            yield
            V(lambda: ve.tensor_scalar(out=mu[:, 1:2], in0=mu[:, 1:2], scalar1=1.0 / 1024, scalar2=LN_EPS, op0=ALU.mult,
                                       op1=ALU.add), r=[T_mu], w=[T_mu])
            A(lambda: se.activation(out=mu[:, 1:2], in_=mu[:, 1:2], func=AF.Ln), r=[T_mu], w=[T_mu])
            A(lambda: se.activation(out=mu[:, 1:2], in_=mu[:, 1:2], func=AF.Exp, scale=-0.5), r=[T_mu], w=[T_mu])
            yield
            V(lambda: ve.tensor_scalar(out=xt, in0=xt, scalar1=mu[:, 1:2], scalar2=None, op0=ALU.mult),
              r=[T_x[t], T_mu], w=[T_x[t]])
            G(lambda: ge.tensor_tensor(out=xt, in0=xt, in1=lnbc[:, 0:1024], op=ALU.mult),
              r=[T_x[t], LNBC], w=[T_x[t]])
            yield
            V(lambda: ve.tensor_tensor(out=xt, in0=xt, in1=lnbc[:, 1024:2048], op=ALU.add),
              r=[T_x[t], LNBC], w=[T_x[t]])
            if l == L - 1:
                tk = Tok("o_y")
                out_toks.append(tk)
                SP.dma(out=y_out[g][t * 128:(t + 1) * 128, :], in_=xt, r=[T_x[t]], w=[tk], owner=T_x[t])

        NSTEP = 7
        ogens = [out_gen(t) for t in range(NT)]
        for i in range(NT + NSTEP - 1):
            for k_ in reversed(range(NSTEP)):
                j = i - k_
                if 0 <= j < NT:
                    next(ogens[j], None)

    ph = 0
    for g in range(2):
        for l in range(L):
            if ph < NPH:
                phase(g, l, ph)
            ph += 1

    evs = []
    for tk in out_toks + list(dbg_outs.values()):
        evs += list(tk.w.items())
    SP._wait(evs)
    es.close()
    return nc


def _consts(core):
    cf = np.zeros((128, 656), np.float32)
    cf[:, 528:656] = np.eye(128, dtype=np.float32)
    s = np.arange(128)[:, None]
    t = np.arange(128)[None, :]
    same = (s // 64) == (t // 64)
    cs = (t // 64) * 64
    midf = cs + 31
    midb = cs + 32
    cf[:, 0:128] = same * ((s <= t).astype(np.float32) - (s <= midf).astype(np.float32))
    cf[:, 128:256] = same * ((s >= t).astype(np.float32) - (s >= midb).astype(np.float32))
    cf[:, 256:384] = same * (s <= t)
    cf[:, 384:512] = same * (s >= t)
    sv = np.arange(128)
    cf[:, 512] = sv < 64
    cf[:, 513] = sv >= 64
    cf[:, 514] = sv <= 31
    cf[:, 515] = (sv >= 64) & (sv <= 95)
    cf[:, 516] = sv < 64
    cf[:, 517] = sv >= 64
    cf[:, 518] = (sv >= 32) & (sv < 64)
    cf[:, 519] = sv >= 96
    cf[:, 520 + (core % 4)] = 1.0
    return cf


def _rope_tab(pos0):
    pos = pos0 + np.arange(1024)
    row = (pos // 64).astype(np.float32)
    col = (pos % 64).astype(np.float32)
    out = np.zeros((1024, 192), np.float32)

    def tab(dim):
        q = dim // 4
        inv = (np.float32(10000.0) ** (-np.arange(q, dtype=np.float32) / np.float32(q))).astype(np.float32)
        ar = row[:, None] * inv[None, :]
        ac = col[:, None] * inv[None, :]
        ang = np.concatenate([ar, ar, ac, ac], axis=-1).astype(np.float32)
        cos, sin = np.cos(ang), np.sin(ang)
        sgn = np.concatenate([-np.ones(q), np.ones(q), -np.ones(q), np.ones(q)]).astype(np.float32)
        return cos.astype(np.float32), (sin * sgn[None, :]).astype(np.float32)

    c, s_ = tab(64)
    out[:, 0:64], out[:, 64:128] = c, s_
    c, s_ = tab(32)
    out[:, 128:160], out[:, 160:192] = c, s_
    return out


_NC_CACHE = {}


def kernel(x_prompt, x_sample, cache_gqa_k, cache_gqa_v, cache_diff_k, cache_diff_v, state_hgrn,
           c, c_ctx, w_ada, b_ada, w_in, gqa_q_norm, gqa_k_norm, diff_lambda, diff_subln,
           hgrn_lower_bounds, hgrn_norm, w_out, ln_g, ln_b, _debug=None):
    f = lambda a: np.ascontiguousarray(np.asarray(a, dtype=np.float32))
    x_prompt, x_sample = f(x_prompt), f(x_sample)
    key = tuple(sorted(_debug)) if _debug else None
    if key not in _NC_CACHE:
        _NC_CACHE[key] = build(debug=_debug)
    nc = _NC_CACHE[key]
    rep = lambda v: np.ascontiguousarray(np.broadcast_to(np.asarray(v, np.float32).reshape(1, -1), (128, np.asarray(v).size)))
    gains = np.stack([np.concatenate([rep(gqa_q_norm[l]), rep(gqa_k_norm[l]), rep(diff_subln[l]), rep(hgrn_norm[l])], axis=1)
                      for l in range(L)])
    dlam = np.stack([rep(np.asarray(diff_lambda[l]).reshape(-1)) for l in range(L)])
    hlb = rep(np.asarray(hgrn_lower_bounds).reshape(-1))
    lnbc = np.stack([np.concatenate([rep(ln_g[l]), rep(ln_b[l])], axis=1) for l in range(L)])
    ident = np.eye(128, dtype=np.float32)
    ba = f(b_ada)
    badaT = np.ascontiguousarray(ba[:, 0:2048].reshape(L, 16, 128).transpose(0, 2, 1))
    badag = np.stack([rep(ba[l, 2048:3072]) for l in range(L)])
    shared = {"w_ada": f(w_ada), "badaT": badaT, "badag": f(badag), "w_in": f(w_in), "w_out": f(w_out),
              "gains": f(gains), "dlam": f(dlam), "hlb": f(hlb), "lnbc": f(lnbc), "ident": ident}
    in_maps = []
    for core in range(8):
        b = core // 4
        p0 = (core % 4) * 1024
        cond = np.stack([np.asarray(c_ctx, np.float32), np.asarray(c, np.float32)[b]])
        condT = np.ascontiguousarray(cond.reshape(2, 8, 128).transpose(0, 2, 1))
        m = dict(shared)
        m.update({
            "xp": f(x_prompt[core * 4:(core + 1) * 4].reshape(1024, 1024)),
            "xs": f(x_sample[b, p0:p0 + 1024]),
            "condT": condT,
            "cgk": f(np.asarray(cache_gqa_k)[b].reshape(L, 256, 128)),
            "cgv": f(np.asarray(cache_gqa_v)[b].reshape(L, 256, 128)),
            "cdk": f(np.asarray(cache_diff_k)[b].reshape(L, 256, 256)),
            "cdv": f(np.asarray(cache_diff_v)[b].reshape(L, 256, 256)),
            "st0": f(np.asarray(state_hgrn)[b]),
            "rope": _rope_tab(p0),
            "cf": _consts(core),
        })
        in_maps.append(m)
    res = run_bass_kernel_spmd(nc, in_maps, core_ids=list(range(8)))
    R = res.results
    y_prompt = np.concatenate([R[i]["yp"].reshape(4, 256, 1024) for i in range(8)], axis=0)
    y_sample = np.stack([np.concatenate([R[b * 4 + j]["ys"] for j in range(4)], axis=0) for b in range(2)])
    ngk = np.concatenate([R[i]["ngk"].reshape(4, L, 256, 2, 64) for i in range(8)], axis=0)
    ngv = np.concatenate([R[i]["ngv"].reshape(4, L, 256, 2, 64) for i in range(8)], axis=0)
    ndk = np.concatenate([R[i]["ndk"].reshape(4, L, 256, 4, 64) for i in range(8)], axis=0)
    ndv = np.concatenate([R[i]["ndv"].reshape(4, L, 256, 4, 64) for i in range(8)], axis=0)
    nst = np.concatenate([R[i]["nst"] for i in range(8)], axis=0)
    outs = (y_prompt.astype(np.float32), y_sample.astype(np.float32), ngk.astype(np.float32), ngv.astype(np.float32),
            ndk.astype(np.float32), ndv.astype(np.float32), nst.astype(np.float32))
    if _debug:
        return outs, [{k: v for k, v in R[i].items() if k.startswith("dbg_")} for i in range(8)]
    return outs
```

```python
import math
import os
import numpy as np
from contextlib import ExitStack
import concourse.bass as bass
import concourse.mybir as mybir
from concourse.bass_utils import run_bass_kernel_spmd

F32 = mybir.dt.float32
BF16 = mybir.dt.bfloat16
AF = mybir.ActivationFunctionType
ALU = mybir.AluOpType
AX = mybir.AxisListType

L = 2
NT = 8
NKT_S = 34
ALPHA = (2 * L) ** 0.25
RMS_EPS = 1e-6
LN_EPS = 1e-5

WSEG = [(0, 768, 0), (768, 1280, 1536), (1280, 1536, 1024), (1536, 1792, 768),
        (1792, 2048, 1280), (2048, 2304, 2048), (2304, 2560, 2560),
        (2560, 3072, 3072), (3072, 3328, 2816), (3328, 3584, 2304)]


class Tok:
    def __init__(self, name):
        self.name = name
        self.w = {}
        self.r = {}
        self.dsem = None
        self.dcnt = 0
        self.alias = []

    def events(self):
        return list(self.w.items()) + list(self.r.items())


class Eng:
    def __init__(self, kb, eng, sem, self_sync=True):
        self.kb = kb
        self.eng = eng
        self.sem = sem
        self.cnt = 0
        self.seen = {}
        self.self_sync = self_sync

    def _wait(self, evs):
        need = {}
        for sem, val in evs:
            if sem is self.sem and not self.self_sync:
                continue
            if self.seen.get(sem, 0) >= val:
                continue
            if need.get(sem, 0) < val:
                need[sem] = val
        for sem, val in need.items():
            self.eng.wait_ge(sem, val)
            self.seen[sem] = val

    def _deps(self, r, w, after):
        evs = []
        for b in r:
            evs += list(b.w.items())
        for b in w:
            evs += b.events()
            for a in b.alias:
                evs += a.events()
        for b in after:
            evs += b.events()
        return evs

    def op(self, fn, r=(), w=(), after=()):
        self._wait(self._deps(r, w, after))
        ins = fn()
        self.cnt += 1
        ins.then_inc(self.sem, 1)
        for b in r:
            b.r[self.sem] = self.cnt
        for b in w:
            b.r = {}
            b.w[self.sem] = self.cnt
        return ins

    def barrier(self):
        if self.cnt > 0 and self.seen.get(self.sem, 0) < self.cnt:
            self.eng.wait_ge(self.sem, self.cnt)
            self.seen[self.sem] = self.cnt

    def dma(self, out, in_, r=(), w=(), after=(), owner=None, **kw):
        self._wait(self._deps(r, w, after))
        own = owner or (w[0] if w else r[0])
        if own.dsem is None:
            own.dsem = self.kb.new_sem("d_" + own.name)
        ins = self.eng.dma_start(out=out, in_=in_, **kw)
        own.dcnt += 16
        ins.then_inc(own.dsem, 16)
        for b in r:
            b.r[own.dsem] = own.dcnt
        for b in w:
            b.r = {}
            b.w[own.dsem] = own.dcnt
        return ins


class KB:
    def __init__(self, nc, es):
        self.nc = nc
        self.es = es
        self.nsem = 0
        self.pe = Eng(self, nc.tensor, self.new_sem("pe"), self_sync=False)
        self.act = Eng(self, nc.scalar, self.new_sem("act"))
        self.dve = Eng(self, nc.vector, self.new_sem("dve"))
        self.pool = Eng(self, nc.gpsimd, self.new_sem("pool"))
        self.sp = Eng(self, nc.sync, self.new_sem("sp"))

    def new_sem(self, name):
        self.nsem += 1
        return self.es.enter_context(self.nc.semaphore("s%d_%s" % (self.nsem, name)))

    def sb(self, name, shape, dt):
        return self.es.enter_context(self.nc.sbuf_tensor("sb_" + name, list(shape), dt))

    def ps(self, name, shape, dt):
        return self.es.enter_context(self.nc.psum_tensor("pp_" + name, list(shape), dt))


class Rot:
    def __init__(self, items):
        self.items = items
        self.i = 0

    def __call__(self):
        it = self.items[self.i % len(self.items)]
        self.i += 1
        return it


def build(debug=None):
    nc = bass.Bass("TRN2", target_bir_lowering=False)
    es = ExitStack()
    kb = KB(nc, es)
    PE, ACT, DVE, POOL, SP = kb.pe, kb.act, kb.dve, kb.pool, kb.sp
    P, A, V, G = PE.op, ACT.op, DVE.op, POOL.op
    te, se, ve, ge = nc.tensor, nc.scalar, nc.vector, nc.gpsimd

    def din(name, shape, dt=F32):
        return nc.dram_tensor(name, list(shape), dt, kind="ExternalInput").ap()

    def dout(name, shape):
        return nc.dram_tensor(name, list(shape), F32, kind="ExternalOutput").ap()

    x_in = [din("xp", [1024, 1024]), din("xs", [1024, 1024])]
    condT_d = din("condT", [2, 128, 8])
    w_ada = din("w_ada", [L, 1024, 3072])
    badaT_d = din("badaT", [L, 128, 16])
    badag_d = din("badag", [L, 128, 1024])
    w_in = din("w_in", [L, 1024, 3584])
    w_out = din("w_out", [L, 1024, 1024])
    gains_d = din("gains", [L, 128, 256])
    dlam_d = din("dlam", [L, 128, 128])
    hlb_d = din("hlb", [128, 1024])
    lnbc_d = din("lnbc", [L, 128, 2048])
    cgk = din("cgk", [L, 256, 128])
    cgv = din("cgv", [L, 256, 128])
    cdk = din("cdk", [L, 256, 256])
    cdv = din("cdv", [L, 256, 256])
    st0 = din("st0", [L, 2, 4, 64, 64])
    rope_d = din("rope", [1024, 192])
    cf_d = din("cf", [128, 656])
    ident_d = din("ident", [128, 128])
    y_out = [dout("yp", [1024, 1024]), dout("ys", [1024, 1024])]
    ngk = dout("ngk", [4, L, 256, 128])
    ngv = dout("ngv", [4, L, 256, 128])
    ndk = dout("ndk", [4, L, 256, 256])
    ndv = dout("ndv", [4, L, 256, 256])
    nst = dout("nst", [4, L, 2, 4, 64, 64])
    ginA = [nc.dram_tensor("ginA%d" % i, [384, 1024], BF16).ap() for i in range(4)]
    ginB = [nc.dram_tensor("ginB%d" % i, [384, 1024], BF16).ap() for i in range(4)]
    goutA = [nc.dram_tensor("goutA%d" % i, [4 * 384, 1024], BF16).ap() for i in range(L)]
    goutB = [nc.dram_tensor("goutB%d" % i, [4 * 384, 1024], BF16).ap() for i in range(L)]
    gin2 = [nc.dram_tensor("ginh%d" % i, [128, 260], F32).ap() for i in range(L)]
    gout2 = [nc.dram_tensor("gouth%d" % i, [512, 260], F32).ap() for i in range(L)]
    winb = [nc.dram_tensor("winb%d" % i, [1024, 3584], BF16).ap() for i in range(L)]
    woutb = [nc.dram_tensor("woutb%d" % i, [1024, 1024], BF16).ap() for i in range(L)]
    wadab = [nc.dram_tensor("wadab%d" % i, [1024, 3072], BF16).ap() for i in range(L)]
    T_winb = [Tok("winb%d" % i) for i in range(L)]
    T_woutb = [Tok("woutb%d" % i) for i in range(L)]
    T_wadab = [Tok("wadab%d" % i) for i in range(L)]
    T_gin = [Tok("gin%d" % i) for i in range(4)]
    T_gout = [Tok("gout%d" % i) for i in range(L)]
    T_gin2 = [Tok("ginh%d" % i) for i in range(L)]
    T_gout2 = [Tok("gouth%d" % i) for i in range(L)]
    dbg_outs = {}

    REG = kb.sb("REG", [128, 28672], BF16)
    win_v = REG[:, :].rearrange("p (c n) -> p c n", c=8)
    kT = REG[:, 0:4352]
    dkT = REG[:, 4352:13056].rearrange("p (h n) -> p h n", h=2)
    VA4 = REG[:, 13056:21896].rearrange("p (k h c) -> p k h c", h=4, c=65)
    lnbc = REG[:, 23944:28040].bitcast(F32)
    wout_v = REG[:, 0:8192].rearrange("p (c n) -> p c n", c=8)
    PTb = [REG[:, 21896 + i * 1024:21896 + (i + 1) * 1024] for i in range(2)]
    ckb = REG[:, 21896:22152].rearrange("p (j c) -> p j c", j=2)
    cdb = REG[:, 22152:22664].rearrange("p (j c) -> p j c", j=2)
    W_IN, LNBC, W_OUT = (Tok(n) for n in ("W_IN", "LNBC", "W_OUT"))
    KTr = [Tok("KT%d" % i) for i in range(5)]
    DKTr = [Tok("DKT%d" % i) for i in range(5)]
    VATr = [[Tok("VAT%d_%d" % (hg, i)) for i in range(5)] for hg in range(2)]
    ALLV = VATr[0] + VATr[1]
    W_IN.alias = KTr + DKTr + ALLV + [LNBC, W_OUT]
    W_OUT.alias = [W_IN] + KTr + DKTr
    for tk_ in KTr + DKTr:
        tk_.alias = [W_IN, W_OUT]
    for tk_ in ALLV:
        tk_.alias = [W_IN]
    LNBC.alias = [W_IN]

    xres = kb.sb("xres", [128, NT, 1024], F32)
    T_x = [Tok("x%d" % t) for t in range(NT)]
    modbc = kb.sb("modbc", [128, 1024], F32)
    modT = kb.sb("modT", [128, 16], F32)
    T_modT = Tok("modT")
    T_mod = Tok("mod")
    concat = kb.sb("concat", [128, NT, 1024], BF16)
    T_cc = [Tok("cc%d" % t) for t in range(NT)]
    qT = kb.sb("qT", [128, 4, 1024], BF16)
    T_qT = [Tok("qT%d" % t) for t in range(NT)]
    qdT = kb.sb("qdT", [128, 2, 1024], BF16)
    T_qdT = [Tok("qdT%d" % t) for t in range(NT)]
    Sloc = kb.sb("Sloc", [128, 2, 16, 2, 64], BF16)
    T_Sloc = [Tok("Sloc%d" % t) for t in range(NT)]
    oacc = kb.sb("oacc", [128, NT, 256], F32)
    T_oacc = [Tok("oacc%d" % t) for t in range(NT)]
    qeT = kb.sb("qeT", [128, 2, NT, 2, 128], BF16)
    T_qeT = [Tok("qeT%d" % t) for t in range(NT)]
    DTt = kb.sb("DTt", [128, 2, 16, 2], F32)
    EMt = kb.sb("EMt", [128, 2, 16, 2], F32)
    GTt = kb.sb("GTt", [128, 2, 16, 2], F32)
    T_stat = [Tok("stat%d" % t) for t in range(NT)]
    cf = kb.sb("cf", [128, 656], F32)
    T_cf = Tok("cf")
    ident = kb.sb("ident", [128, 128], BF16)
    T_id = Tok("ident")
    gn = kb.sb("gn", [128, 256], F32)
    T_gn = Tok("gn")
    lbt = kb.sb("lbt", [128, 512], F32)
    oml = kb.sb("oml", [128, 512], F32)
    T_lb = Tok("lb")
    lamt = kb.sb("lamt", [128, 8], F32)
    T_lam = Tok("lam")
    T_Pb = [Tok("PTb%d" % i) for i in range(2)]
    for tk_ in T_Pb:
        tk_.alias = [W_IN]
    W_IN.alias += T_Pb
    rotP = Rot([(PTb[i], T_Pb[i]) for i in range(2)])
    SCR = kb.sb("SCR", [128, 2560], F32)
    s1024 = [SCR[:, 0:1024], SCR[:, 1024:2048]]
    T_s1024 = [Tok("s1024_%d" % i) for i in range(2)]
    rot1024 = Rot([(s1024[i], T_s1024[i]) for i in range(2)])
    g_d = [SCR[:, 0:512], SCR[:, 512:1024]]
    T_g = [Tok("g%d" % i) for i in range(2)]
    kk_d = [SCR[:, 1024:1536], SCR[:, 1536:2048]]
    T_kkd = [Tok("kk%d" % i) for i in range(2)]
    cT_d = SCR[:, 2048:2560].bitcast(BF16).rearrange("p (c n) -> p c n", c=8)
    T_cTd = Tok("cT")
    qe_d = SCR[:, 2048:2304].bitcast(BF16)
    ke_d = SCR[:, 2304:2560].bitcast(BF16)
    T_qed, T_ked = Tok("qe"), Tok("ke")
    T_s1024[0].alias = list(T_g)
    T_s1024[1].alias = list(T_kkd)
    for tk_ in T_g:
        tk_.alias = [T_s1024[0]]
    for tk_ in T_kkd:
        tk_.alias = [T_s1024[1]]
    T_cTd.alias = [T_qed, T_ked]
    T_qed.alias = [T_cTd]
    T_ked.alias = [T_cTd]
    keT_d = kb.sb("keT", [128, 512], BF16)
    T_keTd = Tok("keT")
    atm_d = [kb.sb("atm%d" % i, [128, 512], BF16) for i in range(2)]
    T_atmd = [Tok("atm%d" % i) for i in range(2)]
    qf2_d = [kb.sb("qf2_%d" % i, [128, 512], BF16) for i in range(2)]
    T_qf2d = [Tok("qf2_%d" % i) for i in range(2)]
    dqf_d = [kb.sb("dqf%d" % i, [128, 256], BF16) for i in range(2)]
    T_dqfd = [Tok("dqf%d" % i) for i in range(2)]
    vb_d = [kb.sb("vb%d" % i, [128, 256], BF16) for i in range(2)]
    T_vbd = [Tok("vb%d" % i) for i in range(2)]
    sgq_d = [kb.sb("sgq%d" % i, [128, 256], F32) for i in range(2)]
    T_sgqd = [Tok("sgq%d" % i) for i in range(2)]
    el_d = kb.sb("el", [128, 16], F32)
    mu_all = kb.sb("mu_all", [128, NT, 4], F32)
    T_mua = [Tok("mu%d" % t) for t in range(NT)]
    T_eld = Tok("el")
    s512 = [kb.sb("s512_%d" % i, [128, 512], F32) for i in range(3)]
    rot512 = Rot([(s512[i], Tok("s512_%d" % i)) for i in range(3)])
    b512 = [kb.sb("b512_%d" % i, [128, 512], BF16) for i in range(2)]
    rotb512 = Rot([(b512[i], Tok("b512_%d" % i)) for i in range(2)])
    s16 = [kb.sb("s16_%d" % i, [128, 16], F32) for i in range(6)]
    rot16 = Rot([(s16[i], Tok("s16_%d" % i)) for i in range(6)])
    hTb = [kb.sb("hT%d" % i, [128, 8, 128], BF16) for i in range(2)]
    rothT = Rot([(hTb[i], Tok("hT%d" % i)) for i in range(2)])
    kvo = [kb.sb("kvo%d" % i, [128, 768], F32) for i in range(1)]
    rotkvo = Rot([(kvo[i], Tok("kvo%d" % i)) for i in range(1)])
    kvbb = [kb.sb("kvb%d" % i, [128, 768], BF16) for i in range(2)]
    rotkvb = Rot([(kvbb[i], Tok("kvb%d" % i)) for i in range(2)])
    ktst = [kb.sb("ktst%d" % i, [128, 3, 128], BF16) for i in range(1)]
    rotktst = Rot([(ktst[i], Tok("ktst%d" % i)) for i in range(1)])
    ropeb = [kb.sb("rope%d" % i, [128, 192], F32) for i in range(1)]
    rotrope = Rot([(ropeb[i], Tok("rope%d" % i)) for i in range(1)])
    Srun = kb.sb("Srun", [128, 2, 2, 64], F32)
    T_Srun = [Tok("Srun0"), Tok("Srun1")]
    Spb = [kb.sb("Sp%d" % i, [128, 2, 64], BF16) for i in range(4)]
    rotSp_d = [Rot([(Spb[2 * d_ + i], Tok("Sp%d" % (2 * d_ + i))) for i in range(2)]) for d_ in range(2)]
    stmp = [kb.sb("stmp%d" % i, [128, 2, 64], F32) for i in range(2)]
    rotstmp_d = [Rot([(stmp[d_], Tok("stmp%d" % d_))]) for d_ in range(2)]
    hx = kb.sb("hx", [128, 2, 130], F32)
    T_hx = Tok("hx")
    hg = kb.sb("hg", [128, 4, 260], F32)
    T_hg = Tok("hg")
    cand = kb.sb("cand", [128, 2, 2, 64], F32)
    T_cand = Tok("cand")
    s0t = kb.sb("s0t", [128, 2, 2, 64], F32)
    T_s0 = Tok("s0t")
    condT = kb.sb("condT", [128, 8], F32)
    T_cond = Tok("condT")
    sc8 = kb.sb("sc8", [128, 8], F32)
    T_sc8 = Tok("sc8")
    screp = kb.sb("screp", [128, 8, 128], BF16)
    T_screp = Tok("screp")
    badaT = kb.sb("badaT", [128, 16], F32)
    T_bada = Tok("bada")
    T_ckb = Tok("ckb")
    T_cdb = Tok("cdb")
    T_ckb.alias = [W_IN, T_Pb[0]]
    T_cdb.alias = [W_IN, T_Pb[0]]
    W_IN.alias += [T_ckb, T_cdb]
    T_Pb[0].alias += [T_ckb, T_cdb]
    rotcT = Rot([(cT_d, T_cTd)])
    nstst = [kb.sb("nstst%d" % i, [128, 2, 64], F32) for i in range(1)]
    rotnst = Rot([(nstst[i], Tok("nstst%d" % i)) for i in range(1)])
    qblk = [kb.sb("qblk%d" % i, [128, 512], BF16) for i in range(1)]
    T_qblk = [Tok("qblk%d" % i) for i in range(1)]
    rotqblk = Rot([(qblk[i], T_qblk[i]) for i in range(1)])
    qgb = [kb.sb("qgb%d" % i, [128, 512], BF16) for i in range(2)]
    T_qgb = [Tok("qgb%d" % i) for i in range(2)]
    print("sbuf bytes remaining:", nc.sbuf_bytes_remaining)

    PSP = [kb.ps("psp%d" % i, [128, 1024], F32) for i in range(2)]
    PS2 = kb.ps("ps2", [128, 1024], F32)
    PB = [PSP[0][:, 0:512], PSP[0][:, 512:1024], PSP[1][:, 0:512], PSP[1][:, 512:1024], PS2[:, 0:512], PS2[:, 512:1024]]
    T_PB = [Tok("PB%d" % i) for i in range(6)]
    PTT = kb.ps("ptt", [128, 2048], BF16)
    PT = [PTT[:, 0:1024], PTT[:, 1024:2048]]
    T_PT = [Tok("PT%d" % i) for i in range(2)]
    PSP3 = PTT[:, :].bitcast(F32)
    PB += [PSP3[:, 0:512], PSP3[:, 512:1024]]
    T_PB += T_PT
    SPAIR = [(PSP[0][:, :], 0, 1), (PSP[1][:, :], 2, 3), (PSP3, 6, 7)]
    rotPB = Rot(list(range(6)))
    rotPT = Rot(list(range(2)))
    rotS = Rot([0, 1, 2])
    rotO = Rot([4, 5])

    def dbg(name, ap, toks, shape):
        if debug is None or name not in debug:
            return
        d = dout("dbg_" + name, shape)
        tk = Tok("dbg_" + name)
        POOL.dma(out=d, in_=ap, r=list(toks), w=[tk])
        dbg_outs[name] = tk

    SP.dma(out=cf[:, :], in_=cf_d, w=[T_cf])
    for i in range(1):
        G(lambda: ge.memset(qblk[i][:, :], 0.0), w=[T_qblk[i]])
    for i in range(2):
        G(lambda: ge.memset(qgb[i][:, :], 0.0), w=[T_qgb[i]])
    POOL.dma(out=ident[:, :], in_=ident_d, w=[T_id])
    LcT = [cf[:, 0:128], cf[:, 128:256]]
    MSK = [cf[:, 256:384], cf[:, 384:512]]
    IND = [cf[:, 512:516], cf[:, 516:520]]
    SEL = cf[:, 520:524]
    IDF = cf[:, 528:656]

    out_toks = []

    def bview(ap, shape, axis):
        return ap.unsqueeze(axis).broadcast_to(list(shape))

    STAGE = int(os.environ.get("KDBG_STAGE", "9"))
    NPH = int(os.environ.get("KDBG_PHASES", "4"))
    SUB = int(os.environ.get("KDBG_SUB", "99"))
    NTL = int(os.environ.get("KDBG_TILES", "8"))

    conv_list = []
    for (dst, src, tk) in ((woutb[0], w_out[0], T_woutb[0]), (winb[1], w_in[1], T_winb[1]),
                           (wadab[1], w_ada[1], T_wadab[1]), (woutb[1], w_out[1], T_woutb[1]),
                           (winb[0], w_in[0], T_winb[0]), (wadab[0], w_ada[0], T_wadab[0])):
        for c in range(8):
            conv_list.append((dst[c * 128:(c + 1) * 128, :], src[c * 128:(c + 1) * 128, :], tk))

    def pump(n):
        for _ in range(n):
            if conv_list:
                dst, src, tk = conv_list.pop(0)
                POOL.dma(out=dst, in_=src, w=[tk])

    def phase(g, l, ph):
        lam_init = 0.8 - 0.6 * math.exp(-0.3 * l)
        nkt = NKT_S if g == 1 else 8

        SP.dma(out=gn[:, :], in_=gains_d[l], w=[T_gn])
        SP.dma(out=badaT[:, :], in_=badaT_d[l], w=[T_bada])
        SP.dma(out=condT[:, :], in_=condT_d[g], w=[T_cond])
        dl, T_dl = rot512()
        SP.dma(out=dl[:, 0:128], in_=dlam_d[l], w=[T_dl])
        dl4 = dl[:, 0:128].rearrange("p (a b x) -> p a b x", a=2, b=2)
        pr, T_pr = rot512()
        V(lambda: ve.tensor_tensor(out=pr[:, 0:64].rearrange("p (a x) -> p a x", a=2), in0=dl4[:, :, 0, :],
                                   in1=dl4[:, :, 1, :], op=ALU.mult), r=[T_dl], w=[T_pr])
        V(lambda: ve.tensor_reduce(out=lamt[:, 2:4], in_=pr[:, 0:64].rearrange("p (a x) -> p a x", a=2),
                                   axis=AX.X, op=ALU.add), r=[T_pr], w=[T_lam])
        A(lambda: se.activation(out=lamt[:, 4:6], in_=lamt[:, 2:4], func=AF.Exp), r=[T_lam], w=[T_lam])
        V(lambda: ve.tensor_tensor(out=lamt[:, 0:1], in0=lamt[:, 4:5], in1=lamt[:, 5:6], op=ALU.subtract),
          r=[T_lam], w=[T_lam])
        V(lambda: ve.tensor_scalar(out=lamt[:, 0:1], in0=lamt[:, 0:1], scalar1=lam_init, scalar2=None, op0=ALU.add),
          r=[T_lam], w=[T_lam])
        V(lambda: ve.tensor_scalar(out=lamt[:, 1:2], in0=lamt[:, 0:1], scalar1=-1.0, scalar2=None, op0=ALU.mult),
          r=[T_lam], w=[T_lam])
        V(lambda: ve.tensor_scalar(out=gn[:, 128:192], in0=gn[:, 128:192], scalar1=(1.0 - lam_init), scalar2=None,
                                   op0=ALU.mult), r=[T_gn], w=[T_gn])
        if l == 0:
            V(lambda: ve.memset(lbt[:, :], 0.0), w=[T_lb])
            V(lambda: ve.memset(oml[:, :], 1.0), w=[T_lb])
        else:
            hl, T_hl = rot1024()
            SP.dma(out=hl[:, :], in_=hlb_d, w=[T_hl])
            V(lambda: ve.tensor_tensor(out=lbt[:, :], in0=hl[:, 512:1024], in1=hl[:, 0:512], op=ALU.subtract),
              r=[T_hl], w=[T_lb])
            A(lambda: se.activation(out=lbt[:, :], in_=lbt[:, :], func=AF.Sigmoid), r=[T_lb], w=[T_lb])
            V(lambda: ve.tensor_scalar(out=oml[:, :], in0=lbt[:, :], scalar1=-1.0, scalar2=1.0, op0=ALU.mult,
                                       op1=ALU.add), r=[T_lb], w=[T_lb])

        if ph == 0:
            wsrc = w_in[l].rearrange("(c p) n -> p c n", p=128)
            for (a, b, pos) in WSEG:
                POOL.dma(out=win_v[:, :, pos:pos + (b - a)], in_=wsrc[:, :, a:b], w=[W_IN])
        else:
            wsrc = winb[l].rearrange("(c p) n -> p c n", p=128)
            for (a, b, pos) in WSEG:
                POOL.dma(out=win_v[:, :, pos:pos + (b - a)], in_=wsrc[:, :, a:b], r=[T_winb[l]], w=[W_IN])

        sg8, T_sg8 = rot16()
        A(lambda: se.activation(out=sg8[:, 0:8], in_=condT[:, :], func=AF.Sigmoid), r=[T_cond], w=[T_sg8])
        V(lambda: ve.tensor_tensor(out=sc8[:, :], in0=condT[:, :], in1=sg8[:, 0:8], op=ALU.mult),
          r=[T_cond, T_sg8], w=[T_sc8])
        V(lambda: ve.tensor_copy(out=screp[:, :, :], in_=bview(sc8[:, :], [128, 8, 128], 2)), r=[T_sc8], w=[T_screp])
        ccflat = concat[:, :, :].rearrange("p t n -> p (t n)")
        for c in range(8):
            slot = ccflat[:, (c % 2) * 3072:(c % 2) * 3072 + 3072]
            tk = T_cc[c % 2]
            if ph == 0:
                POOL.dma(out=slot, in_=w_ada[l, c * 128:(c + 1) * 128, :], w=[tk], after=T_cc)
            else:
                SP.dma(out=slot, in_=wadab[l][c * 128:(c + 1) * 128, :], r=[T_wadab[l]], w=[tk], after=T_cc)
            for j in range(6):
                P(lambda: te.matmul(PB[j], lhsT=screp[:, c, :], rhs=slot[:, j * 512:(j + 1) * 512],
                                    start=(c == 0), stop=(c == 7)), r=[T_screp, tk], w=[T_PB[j]])
        for j in range(4):
            dg, T_dg = rot512()
            V(lambda: ve.tensor_tensor(out=dg[:, :].rearrange("p (c n) -> p c n", c=4),
                                       in0=PB[j].rearrange("p (c n) -> p c n", c=4),
                                       in1=bview(IDF, [128, 4, 128], 1), op=ALU.mult), r=[T_PB[j], T_cf], w=[T_dg])
            V(lambda: ve.tensor_reduce(out=modT[:, j * 4:(j + 1) * 4], in_=dg[:, :].rearrange("p (c n) -> p c n", c=4),
                                       axis=AX.X, op=ALU.add), r=[T_dg], w=[T_modT])
        V(lambda: ve.tensor_tensor(out=modT[:, :], in0=modT[:, :], in1=badaT[:, :], op=ALU.add),
          r=[T_modT, T_bada], w=[T_modT])
        V(lambda: ve.tensor_scalar(out=modT[:, 8:16], in0=modT[:, 8:16], scalar1=1.0, scalar2=None, op0=ALU.add),
          r=[T_modT], w=[T_modT])
        bg, T_bg = rot1024()
        SP.dma(out=bg[:, :], in_=badag_d[l], w=[T_bg])
        for j in range(2):
            V(lambda: ve.tensor_tensor(out=modbc[:, j * 512:(j + 1) * 512], in0=PB[4 + j], in1=bg[:, j * 512:(j + 1) * 512],
                                       op=ALU.add), r=[T_PB[4 + j], T_bg], w=[T_mod])

        if STAGE < 1:
            return
        def rstd_of(sq_ap, T_sq, nh):
            ss, T_ss = rot16()
            V(lambda: ve.tensor_reduce(out=ss[:, 0:nh], in_=sq_ap.rearrange("p (h d) -> p h d", d=64), axis=AX.X,
                                       op=ALU.add), r=[T_sq], w=[T_ss])
            V(lambda: ve.tensor_scalar(out=ss[:, 0:nh], in0=ss[:, 0:nh], scalar1=1.0 / 64, scalar2=RMS_EPS,
                                       op0=ALU.mult, op1=ALU.add), r=[T_ss], w=[T_ss])
            A(lambda: se.activation(out=ss[:, 0:nh], in_=ss[:, 0:nh], func=AF.Ln), r=[T_ss], w=[T_ss])
            A(lambda: se.activation(out=ss[:, 0:nh], in_=ss[:, 0:nh], func=AF.Exp, scale=-0.5), r=[T_ss], w=[T_ss])
            return ss[:, 0:nh], T_ss

        def rope_apply(src, T_src, W, x, cos_ap, sin_ap, T_rope, out_ap, T_out):
            nb = W // (4 * x)
            rot, T_rot = rot512()
            s5 = src.rearrange("p (h j s x) -> p h j s x", j=2, s=2, x=x)
            r5 = rot[:, 0:W].rearrange("p (h j s x) -> p h j s x", j=2, s=2, x=x)
            sn = sin_ap.rearrange("p (j s x) -> p j s x", j=2, s=2)
            for s in range(2):
                V(lambda: ve.tensor_tensor(out=r5[:, :, :, s, :], in0=s5[:, :, :, 1 - s, :],
                                           in1=bview(sn[:, :, s, :], [128, nb, 2, x], 1), op=ALU.mult),
                  r=[T_src, T_rope], w=[T_rot])
            V(lambda: ve.tensor_tensor(out=src.rearrange("p (h c) -> p h c", c=4 * x),
                                       in0=src.rearrange("p (h c) -> p h c", c=4 * x),
                                       in1=bview(cos_ap, [128, nb, 4 * x], 1), op=ALU.mult),
              r=[T_src, T_rope], w=[T_src])
            V(lambda: ve.tensor_tensor(out=out_ap, in0=src, in1=rot[:, 0:W], op=ALU.add), r=[T_src, T_rot], w=[T_out])

        def rmsnorm_ps(src_ps, T_ps, nh, gain_ap, out_ap, T_out, rope=None):
            W = nh * 64
            sq, T_sq = rot512()
            raw, T_raw = rot512()
            A(lambda: se.activation(out=sq[:, 0:W], in_=src_ps, func=AF.Square), r=[T_ps], w=[T_sq])
            A(lambda: se.activation(out=raw[:, 0:W], in_=src_ps, func=AF.Copy), r=[T_ps], w=[T_raw])
            rs, T_rs = rstd_of(sq[:, 0:W], T_sq, nh)
            raw3 = raw[:, 0:W].rearrange("p (h d) -> p h d", d=64)
            V(lambda: ve.tensor_tensor(out=raw3, in0=raw3, in1=bview(rs, [128, nh, 64], 2), op=ALU.mult),
              r=[T_raw, T_rs], w=[T_raw])
            if rope is None:
                V(lambda: ve.tensor_tensor(out=out_ap.rearrange("p (h d) -> p h d", d=64), in0=raw3,
                                           in1=bview(gain_ap, [128, nh, 64], 1), op=ALU.mult),
                  r=[T_raw, T_gn], w=[T_out])
            else:
                V(lambda: ve.tensor_tensor(out=raw3, in0=raw3, in1=bview(gain_ap, [128, nh, 64], 1), op=ALU.mult),
                  r=[T_raw, T_gn], w=[T_raw])
                cos_ap, sin_ap, T_rope = rope
                rope_apply(raw[:, 0:W], T_raw, W, 16, cos_ap, sin_ap, T_rope, out_ap, T_out)

        def gate_block(pb, t, lo):
            sg, T_sg = rot512()
            A(lambda: se.activation(out=sg[:, :], in_=PB[pb], func=AF.Sigmoid), r=[T_PB[pb]], w=[T_sg])
            V(lambda: ve.tensor_tensor(out=concat[:, t, lo:lo + 512], in0=sg[:, :], in1=PB[pb], op=ALU.mult),
              r=[T_sg, T_PB[pb]], w=[T_cc[t]])

        def proj_gen(t):
            p_ = t % 2
            if l == 0:
                SP.dma(out=xres[:, t, :], in_=x_in[g][t * 128:(t + 1) * 128, :], w=[T_x[t]])
            if g == 1:
                rb, T_rb = rotrope()
                SP.dma(out=rb[:, :], in_=rope_d[t * 128:(t + 1) * 128, :], w=[T_rb])
                rpA = (rb[:, 0:64], rb[:, 64:128], T_rb)
                rpD = (rb[:, 128:160], rb[:, 160:192], T_rb)
            for c in range(8):
                P(lambda: te.transpose(out=PS2[:, c * 128:(c + 1) * 128], in_=xres[:, t, c * 128:(c + 1) * 128],
                                       identity=IDF), r=[T_x[t], T_cf], w=[T_PB[4], T_PB[5]])
            hT, T_hT = rothT()
            for c in range(8):
                V(lambda: ve.tensor_scalar(out=hT[:, c, :], in0=PS2[:, c * 128:(c + 1) * 128], scalar1=modT[:, 8 + c:9 + c],
                                           scalar2=modT[:, c:c + 1], op0=ALU.mult, op1=ALU.add),
                  r=[T_PB[4], T_PB[5], T_modT], w=[T_hT])

            def mm_block(blk):
                pb = rotPB()
                for c in range(8):
                    P(lambda: te.matmul(PB[pb], lhsT=hT[:, c, :], rhs=win_v[:, c, blk * 512:(blk + 1) * 512],
                                        start=(c == 0), stop=(c == 7)), r=[T_hT, W_IN], w=[T_PB[pb]])
                return pb

            pb = mm_block(0)
            qf, T_qf = rotb512()
            rmsnorm_ps(PB[pb], T_PB[pb], 8, gn[:, 0:64], qf[:, :], T_qf, rope=(rpA if g == 1 else None))
            qf2, T_qf2 = qf2_d[p_], T_qf2d[p_]
            G(lambda: ge.tensor_copy(out=qf2[:, :].rearrange("p (r g d) -> p r g d", r=4, g=2),
                                     in_=qf[:, :].rearrange("p (g r d) -> p r g d", g=2, r=4)), r=[T_qf], w=[T_qf2])
            yield
            ko, T_ko = rotkvo()
            pb1 = mm_block(1)
            rmsnorm_ps(PB[pb1][:, 0:128], T_PB[pb1], 2, gn[:, 64:128], ko[:, 0:128], T_ko,
                       rope=(rpA if g == 1 else None))
            A(lambda: se.activation(out=ko[:, 128:256], in_=PB[pb1][:, 128:256], func=AF.Copy),
              r=[T_PB[pb1]], w=[T_ko])
            if g == 0:
                A(lambda: se.activation(out=ko[:, 256:512], in_=PB[pb1][:, 256:512], func=AF.Copy),
                  r=[T_PB[pb1]], w=[T_ko])
            else:
                dkr, T_dkr = rot512()
                A(lambda: se.activation(out=dkr[:, 0:256], in_=PB[pb1][:, 256:512], func=AF.Copy),
                  r=[T_PB[pb1]], w=[T_dkr])
                rope_apply(dkr[:, 0:256], T_dkr, 256, 8, rpD[0], rpD[1], rpD[2], ko[:, 256:512], T_ko)
            pb2 = mm_block(2)
            A(lambda: se.activation(out=ko[:, 512:768], in_=PB[pb2][:, 256:512], func=AF.Copy),
              r=[T_PB[pb2]], w=[T_ko])
            dqf, T_dqf = dqf_d[p_], T_dqfd[p_]
            if g == 0:
                A(lambda: se.activation(out=dqf[:, 0:256], in_=PB[pb2][:, 0:256], func=AF.Copy),
                  r=[T_PB[pb2]], w=[T_dqf])
            else:
                dqr, T_dqr = rot512()
                A(lambda: se.activation(out=dqr[:, 0:256], in_=PB[pb2][:, 0:256], func=AF.Copy),
                  r=[T_PB[pb2]], w=[T_dqr])
                rope_apply(dqr[:, 0:256], T_dqr, 256, 8, rpD[0], rpD[1], rpD[2], dqf[:, 0:256], T_dqf)
            if g == 0:
                bb, hf = t // 2, t % 2
                tk = Tok("o_kv")
                out_toks.append(tk)
                for (dst, lo, hi) in ((ngk, 0, 128), (ngv, 128, 256), (ndk, 256, 512), (ndv, 512, 768)):
                    SP.dma(out=dst[bb, l, hf * 128:(hf + 1) * 128, :], in_=ko[:, lo:hi], r=[T_ko], w=[tk], owner=T_ko)
            kb_, T_kb = rotkvb()
            G(lambda: ge.tensor_copy(out=kb_[:, :], in_=ko[:, :]), r=[T_ko], w=[T_kb])
            yield
            pb = mm_block(3)
            gate_block(pb, t, 0)
            pb = mm_block(4)
            gate_block(pb, t, 512)
            yield
            pb5 = mm_block(5)
            sgq, T_sgq = sgq_d[p_], T_sgqd[p_]
            A(lambda: se.activation(out=sgq[:, 0:256], in_=PB[pb5][:, 0:256], func=AF.Sigmoid),
              r=[T_PB[pb5]], w=[T_sgq])
            V(lambda: ve.tensor_tensor(out=sgq[:, 0:256], in0=sgq[:, 0:256], in1=PB[pb5][:, 0:256], op=ALU.mult),
              r=[T_sgq, T_PB[pb5]], w=[T_sgq])
            vb, T_vb = vb_d[p_], T_vbd[p_]
            A(lambda: se.activation(out=vb[:, 0:256], in_=PB[pb5][:, 256:512], func=AF.Copy),
              r=[T_PB[pb5]], w=[T_vb])
            pb6 = mm_block(6)
            sig, T_sig = g_d[p_], T_g[p_]
            kk, T_kk = kk_d[p_], T_kkd[p_]
            A(lambda: se.activation(out=sig[:, :], in_=PB[pb6], func=AF.Sigmoid), r=[T_PB[pb6]], w=[T_sig])
            V(lambda: ve.tensor_tensor(out=sig[:, :], in0=sig[:, :], in1=oml[:, :], op=ALU.mult),
              r=[T_sig, T_lb], w=[T_sig])
            V(lambda: ve.tensor_tensor(out=kk[:, :], in0=oml[:, :], in1=sig[:, :], op=ALU.subtract),
              r=[T_sig, T_lb], w=[T_kk])
            V(lambda: ve.tensor_tensor(out=sig[:, :], in0=sig[:, :], in1=lbt[:, :], op=ALU.add),
              r=[T_sig, T_lb], w=[T_sig])
            V(lambda: ve.tensor_scalar(out=sig[:, :], in0=sig[:, :], scalar1=1e-6, scalar2=None, op0=ALU.max),
              r=[T_sig], w=[T_sig])
            A(lambda: se.activation(out=sig[:, :], in_=sig[:, :], func=AF.Ln), r=[T_sig], w=[T_sig])
            yield
            pt = rotPT()
            for r_ in range(4):
                P(lambda: te.transpose(out=PT[pt][:, r_ * 128:(r_ + 1) * 128], in_=qf2[:, r_ * 128:(r_ + 1) * 128],
                                       identity=ident[:, :]), r=[T_qf2, T_id], w=[T_PT[pt]])
            A(lambda: se.activation(out=qT[:, :, t * 128:(t + 1) * 128],
                                    in_=PT[pt][:, 0:512].rearrange("p (r n) -> p r n", r=4), func=AF.Copy),
              r=[T_PT[pt]], w=[T_qT[t]])
            pg = rotPB()
            for d in range(2):
                P(lambda: te.matmul(PB[pg][:, d * 256:(d + 1) * 256], lhsT=LcT[d], rhs=sig[:, d * 256:(d + 1) * 256],
                                    start=True, stop=True), r=[T_cf, T_sig], w=[T_PB[pg]])
            pst = rotPB()
            for d in range(2):
                for hp in range(2):
                    P(lambda: te.matmul(PB[pst][:, d * 8 + hp:d * 8 + 8:2], lhsT=sig[:, d * 256 + hp * 128:d * 256 + hp * 128 + 128],
                                        rhs=IND[d], start=True, stop=True), r=[T_cf, T_sig], w=[T_PB[pst]])
            pt = rotPT()
            srcs_ = [(kb_[:, 0:128], T_kb), (kb_[:, 256:384], T_kb), (kb_[:, 384:512], T_kb),
                     (dqf[:, 0:128], T_dqf), (dqf[:, 128:256], T_dqf)]
            for i, (sap, stk) in enumerate(srcs_):
                P(lambda: te.transpose(out=PT[pt][:, i * 128:(i + 1) * 128], in_=sap, identity=ident[:, :]),
                  r=[stk, T_id], w=[T_PT[pt]])
            e1, T_e1 = rot512()
            e2, T_e2 = rot512()
            A(lambda: se.activation(out=e1[:, :], in_=PB[pg], func=AF.Exp), r=[T_PB[pg]], w=[T_e1])
            A(lambda: se.activation(out=e2[:, :], in_=PB[pg], func=AF.Exp, scale=-1.0), r=[T_PB[pg]], w=[T_e2])
            qe, T_qe = qe_d, T_qed
            ke, T_ke = ke_d, T_ked
            V(lambda: ve.tensor_tensor(out=qe[:, :].rearrange("p (d n) -> p d n", d=2),
                                       in0=e1[:, :].rearrange("p (d n) -> p d n", d=2),
                                       in1=bview(sgq[:, 0:256], [128, 2, 256], 1), op=ALU.mult),
              r=[T_e1, T_sgq], w=[T_qe])
            V(lambda: ve.tensor_tensor(out=ke[:, :], in0=e2[:, :], in1=kk[:, :], op=ALU.mult),
              r=[T_e2, T_kk], w=[T_ke])
            ks, T_ks = rotktst()
            A(lambda: se.activation(out=ks[:, :, :], in_=PT[pt][:, 0:384].rearrange("p (a n) -> p a n", a=3),
                                    func=AF.Copy), r=[T_PT[pt]], w=[T_ks])
            A(lambda: se.activation(out=qdT[:, :, t * 128:(t + 1) * 128],
                                    in_=PT[pt][:, 384:640].rearrange("p (a n) -> p a n", a=2), func=AF.Copy),
              r=[T_PT[pt]], w=[T_qdT[t]])
            SP.dma(out=ginA[ph].rearrange("(a p) n -> p a n", p=128)[:, 0:3, t * 128:(t + 1) * 128], in_=ks[:, :, :],
                   r=[T_ks], w=[T_gin[ph]], owner=T_ks)
            SP.dma(out=ginB[ph][0:128, :].rearrange("r (q c) -> (r q) c", c=128)[t * 128:(t + 1) * 128, :],
                   in_=kb_[:, 128:256], r=[T_kb], w=[T_gin[ph]], owner=T_kb)
            SP.dma(out=ginB[ph][128:384, :].rearrange("r (q c) -> (r q) c", c=256)[t * 128:(t + 1) * 128, :],
                   in_=kb_[:, 512:768], r=[T_kb], w=[T_gin[ph]], owner=T_kb)
            st, T_st = rot16()
            A(lambda: se.activation(out=st[:, :], in_=PB[pst][:, 0:16], func=AF.Copy), r=[T_PB[pst]], w=[T_st])
            st4 = st[:, :].rearrange("p (d j c) -> p d j c", d=2, j=2)
            ex, T_ex = rot16()
            A(lambda: se.activation(out=ex[:, :], in_=st[:, :], func=AF.Exp), r=[T_st], w=[T_ex])
            ex4 = ex[:, :].rearrange("p (d j c) -> p d j c", d=2, j=2)
            el, T_el = el_d, T_eld
            V(lambda: ve.tensor_tensor(out=el[:, 0:8].rearrange("p (d c) -> p d c", d=2), in0=st4[:, :, 0, :],
                                       in1=st4[:, :, 1, :], op=ALU.subtract), r=[T_st], w=[T_el])
            A(lambda: se.activation(out=el[:, 0:8], in_=el[:, 0:8], func=AF.Exp), r=[T_el], w=[T_el])
            dst = lambda X: X[:, :, 2 * t:2 * t + 2, :].rearrange("p d c h -> p d (c h)")
            V(lambda: ve.tensor_copy(out=dst(DTt), in_=ex4[:, :, 0, :]), r=[T_ex], w=[T_stat[t]])
            V(lambda: ve.tensor_copy(out=dst(EMt), in_=ex4[:, :, 1, :]), r=[T_ex], w=[T_stat[t]])
            V(lambda: ve.tensor_copy(out=dst(GTt), in_=st4[:, :, 0, :]), r=[T_st], w=[T_stat[t]])
            yield
            pt = rotPT()
            for i in range(4):
                P(lambda: te.transpose(out=PT[pt][:, i * 128:(i + 1) * 128], in_=qe[:, i * 128:(i + 1) * 128],
                                       identity=ident[:, :]), r=[T_qe, T_id], w=[T_PT[pt]])
            for i in range(4):
                P(lambda: te.transpose(out=PT[pt][:, 512 + i * 128:512 + (i + 1) * 128],
                                       in_=ke[:, i * 128:(i + 1) * 128], identity=ident[:, :]),
                  r=[T_ke, T_id], w=[T_PT[pt]])
            A(lambda: se.activation(out=qeT[:, :, t, :, :], in_=PT[pt][:, 0:512].rearrange("p (d h n) -> p d h n", d=2, h=2),
                                    func=AF.Copy), r=[T_PT[pt]], w=[T_qeT[t]])
            keT, T_keT = keT_d, T_keTd
            A(lambda: se.activation(out=keT[:, :], in_=PT[pt][:, 512:1024], func=AF.Copy), r=[T_PT[pt]], w=[T_keT])
            keT4 = keT[:, :].rearrange("p (d h n) -> p d h n", d=2, h=2)
            yield
            atms = []
            for d in range(2):
                pa = rotPB()
                for h in (0, 2, 1, 3):
                    hp, hh = h // 2, h % 2
                    if h in (0, 1):
                        PE.barrier()
                    P(lambda: te.matmul(PB[pa][:, h * 128:(h + 1) * 128], lhsT=keT4[hh * 64:(hh + 1) * 64, d, hp, :],
                                        rhs=qeT[hh * 64:(hh + 1) * 64, d, t, hp, :], start=True, stop=True),
                      r=[T_keT, T_qeT[t]], w=[T_PB[pa]])
                atm, T_atm = atm_d[d], T_atmd[d]
                G(lambda: ge.memset(atm[:, :], 0.0), w=[T_atm])
                for h in range(4):
                    V(lambda: ve.copy_predicated(out=atm[:, h * 128:(h + 1) * 128], mask=MSK[d].bitcast(mybir.dt.uint32),
                                                 data=PB[pa][:, h * 128:(h + 1) * 128]),
                      r=[T_PB[pa], T_cf, T_atm], w=[T_atm])
                atms.append((atm, T_atm))
            yield
            pi = rotPB()
            PE.barrier()
            for h in range(4):
                for d in range(2):
                    atm, T_atm = atms[d]
                    P(lambda: te.matmul(PB[pi][:, h * 64:(h + 1) * 64], lhsT=atm[:, h * 128:(h + 1) * 128],
                                        rhs=vb[:, h * 64:(h + 1) * 64], start=(d == 0), stop=(d == 1),
                                        skip_group_check=True), r=[T_atm, T_vb], w=[T_PB[pi]])
            A(lambda: se.activation(out=oacc[:, t, :], in_=PB[pi][:, 0:256], func=AF.Copy), r=[T_PB[pi]], w=[T_oacc[t]])
            for cc in range(2):
                PE.barrier()
                for d in range(2):
                    for hp in range(2):
                        idx = (d * 2 + cc) * 2 + hp
                        P(lambda: te.matmul(PS2[:, idx * 128:(idx + 1) * 128],
                                            lhsT=ke[cc * 64:(cc + 1) * 64, d * 256 + hp * 128:d * 256 + hp * 128 + 128],
                                            rhs=vb[cc * 64:(cc + 1) * 64, hp * 128:(hp + 1) * 128], start=True, stop=True),
                          r=[T_ke, T_vb], w=[T_PB[4], T_PB[5]])
            PE.barrier()
            for hh in range(2):
                rows = slice(hh * 64, (hh + 1) * 64)
                V(lambda: ve.tensor_tensor(
                    out=Sloc[rows, :, 2 * t:2 * t + 2, :, :].rearrange("p d c h v -> p d (c h) v"),
                    in0=PS2[rows, :].rearrange("p (d c n) -> p d c n", d=2, c=4)[:, :, :, hh * 64:(hh + 1) * 64],
                    in1=bview(el[rows, 0:8].rearrange("p (d c) -> p d c", d=2), [64, 2, 4, 64], 3), op=ALU.mult),
                  r=[T_PB[4], T_PB[5], T_el], w=[T_Sloc[t]])

        gens = [proj_gen(t) for t in range(NTL)]
        for i in range(NTL + 1):
            for k_ in range(4):
                if i < NTL:
                    next(gens[i], None)
                if i >= 1:
                    next(gens[i - 1], None)
            pump(3)

        if STAGE < 2:
            return
        G(lambda: ge.memset(VA4[:, :, :, 64:65], 1.0), w=ALLV)
        if g == 1:
            POOL.dma(out=ckb[:, :, :], in_=cgk[l].rearrange("(j p) c -> p j c", p=128), w=[T_ckb])
            POOL.dma(out=cdb[:, :, :], in_=cdk[l].rearrange("(j p) c -> p j c", p=128), w=[T_cdb])
            pt = rotPT()
            for j in range(2):
                P(lambda: te.transpose(out=PT[pt][:, j * 128:(j + 1) * 128], in_=ckb[:, j, :], identity=ident[:, :]),
                  r=[T_ckb, T_id], w=[T_PT[pt]])
            for hp in range(2):
                for j in range(2):
                    P(lambda: te.transpose(out=PT[pt][:, 256 + (hp * 2 + j) * 128:256 + (hp * 2 + j + 1) * 128],
                                           in_=cdb[:, j, hp * 128:(hp + 1) * 128], identity=ident[:, :]),
                      r=[T_cdb, T_id], w=[T_PT[pt]])
            A(lambda: se.activation(out=kT[:, 0:256], in_=PT[pt][:, 0:256], func=AF.Copy), r=[T_PT[pt]], w=[KTr[0]])
            A(lambda: se.activation(out=dkT[:, :, 0:256], in_=PT[pt][:, 256:768].rearrange("p (h n) -> p h n", h=2),
                                    func=AF.Copy), r=[T_PT[pt]], w=[DKTr[0]])
            for gg in range(2):
                POOL.dma(out=VA4[:, 0:2, gg, 0:64], in_=cgv[l].rearrange("(j p) (g d) -> p j g d", p=128, g=2)[:, :, gg, :], w=[VATr[0][0]])
            ins = POOL
            ins._wait(ins._deps([T_gin[ph]], [T_gout[l]], []))
            T_gout[l].r = {}
            for (gi_, go_) in ((ginA[ph], goutA[l]), (ginB[ph], goutB[l])):
                ci = ge.collective_compute("AllGather", ALU.bypass, replica_groups=[[0, 1, 2, 3], [4, 5, 6, 7]],
                                           ins=[gi_.opt()], outs=[go_.opt()])
                csem = kb.new_sem("coll")
                ci.then_inc(csem, 1)
                T_gin[ph].r[csem] = 1
                T_gout[l].w[csem] = 1
            srcs = [(goutA[l][r_ * 384:(r_ + 1) * 384, :], goutB[l][r_ * 384:(r_ + 1) * 384, :], 256 + r_ * 1024,
                     2 + r_ * 8, T_gout[l]) for r_ in range(4)]
        else:
            srcs = [(ginA[ph], ginB[ph], 0, 0, T_gin[ph])]
        for ri, (src, srcB, koff, kt0, tk) in enumerate(srcs):
            SP.dma(out=kT[:, koff:koff + 1024], in_=src[0:128, :], r=[tk], w=[KTr[1 + ri]])
            vsrc = srcB[0:128, :].rearrange("r (q c) -> (r q) c", c=128).rearrange("(j p) (g d) -> p j g d", p=128, g=2)
            for gg in range(2):
                SP.dma(out=VA4[:, kt0:kt0 + 8, gg, 0:64], in_=vsrc[:, :, gg, :], r=[tk], w=[VATr[0][1 + ri]])
        for ri, (src, srcB, koff, kt0, tk) in enumerate(srcs):
            SP.dma(out=dkT[:, :, koff:koff + 1024], in_=src[128:384, :].rearrange("(h p) n -> p h n", p=128),
                   r=[tk], w=[DKTr[1 + ri]])

        def rg_of(kt):
            if g == 0:
                return 1
            return 0 if kt < 2 else 1 + (kt - 2) // 8

        if STAGE < 3:
            return
        def diff_v_loads(heads):
            if g == 1:
                for h in heads:
                    POOL.dma(out=VA4[:, 0:2, h, 0:64], in_=cdv[l].rearrange("(j p) (h d) -> p j h d", p=128, h=4)[:, :, h, :],
                             w=[VATr[h // 2][0]])
            for ri, (src, srcB, koff, kt0, tk) in enumerate(srcs):
                vsrc = srcB[128:384, :].rearrange("r (q c) -> (r q) c", c=256).rearrange("(j p) (h d) -> p j h d", p=128, h=4)
                for h in heads:
                    SP.dma(out=VA4[:, kt0:kt0 + 8, h, 0:64], in_=vsrc[:, :, h, :], r=[tk], w=[VATr[h // 2][1 + ri]])

        diff_v_loads((2, 3))

        def prefix_step(d, c, emit_inter):
            t, cc = c // 2, c % 2
            sr = Srun[:, d, :, :]
            if emit_inter:
                sp, T_sp = rotSp_d[d]()
                V(lambda: ve.tensor_tensor(out=sp[:, :, :], in0=sr, in1=bview(EMt[:, d, c, :], [128, 2, 64], 2),
                                           op=ALU.mult), r=[T_Srun[d], T_stat[t]], w=[T_sp])
            tm, T_tm = rotstmp_d[d]()
            V(lambda: ve.tensor_tensor(out=tm[:, :, :], in0=sr, in1=bview(DTt[:, d, c, :], [128, 2, 64], 2), op=ALU.mult),
              r=[T_Srun[d], T_stat[t]], w=[T_tm])
            V(lambda: ve.tensor_tensor(out=sr, in0=tm[:, :, :], in1=Sloc[:, d, c, :, :], op=ALU.add),
              r=[T_tm, T_Sloc[t]], w=[T_Srun[d]])
            if emit_inter:
                return sp, T_sp
            return None

        def run_dir(d, tiles, emit_inter=True):
            order = tiles if d == 0 else tiles[::-1]
            for t in order:
                chunks = [2 * t, 2 * t + 1] if d == 0 else [2 * t + 1, 2 * t]
                pi = None
                for c in chunks:
                    res = prefix_step(d, c, emit_inter)
                    if not emit_inter:
                        continue
                    sp, T_sp = res
                    cc = c % 2
                    if pi is None:
                        pi = rotPB()
                    for h in (0, 2, 1, 3):
                        hp, hh = h // 2, h % 2
                        if h in (0, 1):
                            PE.barrier()
                        P(lambda: te.matmul(PB[pi][cc * 64:(cc + 1) * 64, h * 64:(h + 1) * 64],
                                            lhsT=qeT[hh * 64:(hh + 1) * 64, d, t, hp, cc * 64:(cc + 1) * 64],
                                            rhs=sp[hh * 64:(hh + 1) * 64, hp, :], start=True, stop=True),
                          r=[T_qeT[t], T_sp], w=[T_PB[pi]])
                if emit_inter:
                    V(lambda: ve.tensor_tensor(out=oacc[:, t, :], in0=oacc[:, t, :], in1=PB[pi][:, 0:256], op=ALU.add),
                      r=[T_PB[pi], T_oacc[t]], w=[T_oacc[t]])
                yield

        def run_both(tiles):
            gs = [run_dir(0, tiles), run_dir(1, tiles)]
            live = True
            while live:
                live = False
                for g_ in gs:
                    try:
                        next(g_)
                        live = True
                    except StopIteration:
                        pass

        st_view = lambda ap: ap.rearrange("(hp hh) k v -> (hh k) hp v", hh=2)
        def sample_pass2_preamble():
            st_view = lambda ap: ap.rearrange("(hp hh) k v -> (hh k) hp v", hh=2)
            tiles = list(range(NT))
            for d in range(2):
                V(lambda: ve.memset(Srun[:, d, :, :], 0.0), w=[T_Srun[d]])
                for _ in run_dir(d, tiles, emit_inter=False):
                    pass
                V(lambda: ve.tensor_copy(out=hx[:, d, 0:128].rearrange("p (h v) -> p h v", h=2), in_=Srun[:, d, :, :]),
                  r=[T_Srun[d]], w=[T_hx])
                V(lambda: ve.tensor_reduce(out=hx[:, d, 128:130], in_=GTt[:, d, :, :].rearrange("p c h -> p h c"),
                                           axis=AX.X, op=ALU.add), r=T_stat, w=[T_hx])
            A(lambda: se.activation(out=hx[:, :, 128:130], in_=hx[:, :, 128:130], func=AF.Exp), r=[T_hx], w=[T_hx])
            SP.dma(out=gin2[l], in_=hx[:, :, :].rearrange("p d n -> p (d n)"), r=[T_hx], w=[T_gin2[l]], owner=T_hx)

        def sample_pass2_preamble_b():
            st_view = lambda ap: ap.rearrange("(hp hh) k v -> (hh k) hp v", hh=2)
            POOL._wait(POOL._deps([T_gin2[l]], [T_gout2[l]], []))
            ci = ge.collective_compute("AllGather", ALU.bypass, replica_groups=[[0, 1, 2, 3], [4, 5, 6, 7]],
                                       ins=[gin2[l].opt()], outs=[gout2[l].opt()])
            csem = kb.new_sem("collh")
            ci.then_inc(csem, 1)
            T_gin2[l].r[csem] = 1
            T_gout2[l].r = {}
            T_gout2[l].w[csem] = 1
            SP.dma(out=hg[:, :, :], in_=gout2[l].rearrange("(r p) n -> p r n", p=128), r=[T_gout2[l]], w=[T_hg])
            SP.dma(out=s0t[:, 0, :, :], in_=st_view(st0[l, 0]), w=[T_s0])
            SP.dma(out=s0t[:, 1, :, :], in_=st_view(st0[l, 1]), w=[T_s0])
            for d in range(2):
                R = cand[:, d, :, :]
                V(lambda: ve.tensor_copy(out=R, in_=s0t[:, d, :, :]), r=[T_s0], w=[T_cand])
                V(lambda: ve.memset(Srun[:, d, :, :], 0.0), w=[T_Srun[d]])
                ranks = [0, 1, 2, 3] if d == 0 else [3, 2, 1, 0]
                for p_ in ranks:
                    V(lambda: ve.scalar_tensor_tensor(out=Srun[:, d, :, :], in0=R, scalar=SEL[:, p_:p_ + 1],
                                                      in1=Srun[:, d, :, :], op0=ALU.mult, op1=ALU.add),
                      r=[T_cand, T_cf], w=[T_Srun[d]])
                    Sp_ = hg[:, p_, d * 130:d * 130 + 128].rearrange("p (h v) -> p h v", h=2)
                    Dp_ = hg[:, p_, d * 130 + 128:d * 130 + 130]
                    V(lambda: ve.tensor_tensor(out=R, in0=R, in1=bview(Dp_, [128, 2, 64], 2), op=ALU.mult),
                      r=[T_cand, T_hg], w=[T_cand])
                    V(lambda: ve.tensor_tensor(out=R, in0=R, in1=Sp_, op=ALU.add), r=[T_cand, T_hg], w=[T_cand])

        if g == 1:
            sample_pass2_preamble()

        def keytiles(t):
            if g == 1:
                return list(range(NKT_S))
            bb = t // 2
            return [2 * bb, 2 * bb + 1]

        def attn_unit(t, qk_fn, pv_fn, scale):
            kts = keytiles(t)
            npair = len(kts) // 2
            ob = rotO()
            sps = [rotS() for _ in range(npair)]

            def qk_pair(i):
                for j in range(2):
                    qk_fn(SPAIR[sps[i]][1 + j], kts[2 * i + j])

            for i in range(min(2, npair)):
                qk_pair(i)
            for i in range(npair):
                psp, b0, b1 = SPAIR[sps[i]]
                pbuf, T_pb = rotP()
                A(lambda: se.activation(out=pbuf, in_=psp, func=AF.Exp, scale=scale),
                  r=[T_PB[b0], T_PB[b1]], w=[T_pb])
                if i + 2 < npair:
                    qk_pair(i + 2)
                pv_fn(ob, pbuf, T_pb, (kts[2 * i], kts[2 * i + 1]), i == 0, i == npair - 1)
            osb, T_osb = rot512()
            A(lambda: se.activation(out=osb[0:65, :], in_=PB[ob][0:65, :], func=AF.Copy), r=[T_PB[ob]], w=[T_osb])
            for j in range(4):
                P(lambda: te.transpose(out=PB[ob][:, j * 65:(j + 1) * 65], in_=osb[0:65, j * 128:(j + 1) * 128],
                                       identity=IDF[0:65, 0:65]), r=[T_osb, T_cf], w=[T_PB[ob]])
            return ob

        for t in range(NT):
            for gg in range(2):
                rows = slice(gg * 64, (gg + 1) * 64)
                G(lambda: ge.tensor_copy(out=qgb[gg][rows, :].rearrange("p (r n) -> p r n", r=4),
                                         in_=qT[rows, :, t * 128:(t + 1) * 128]), r=[T_qT[t]], w=[T_qgb[gg]])

                def qk_fn(sbk, kt):
                    P(lambda: te.matmul(PB[sbk], lhsT=kT[:, kt * 128:(kt + 1) * 128], rhs=qgb[gg][:, :],
                                        start=True, stop=True), r=[KTr[rg_of(kt)], T_qgb[gg]], w=[T_PB[sbk]])

                def pv_fn(ob, pbuf, T_pb, ktp, first, last):
                    for j in range(2):
                        P(lambda: te.matmul(PB[ob][0:65, :], lhsT=VA4[:, ktp[j], gg, :], rhs=pbuf[:, j * 512:(j + 1) * 512],
                                            start=(first and j == 0), stop=(last and j == 1)),
                          r=[T_pb, VATr[0][rg_of(ktp[j])]], w=[T_PB[ob]])

                ob = attn_unit(t, qk_fn, pv_fn, 0.125)
                o3 = PB[ob][:, 0:260].rearrange("p (r c) -> p r c", c=65)
                rs, T_rs = rot16()
                V(lambda: ve.reciprocal(out=rs[:, 0:4], in_=o3[:, :, 64]), r=[T_PB[ob]], w=[T_rs])
                tmp, T_tmp = rot512()
                V(lambda: ve.tensor_tensor(out=tmp[:, 0:256].rearrange("p (r d) -> p r d", d=64), in0=o3[:, :, 0:64],
                                           in1=bview(rs[:, 0:4], [128, 4, 64], 2), op=ALU.mult),
                  r=[T_PB[ob], T_rs], w=[T_tmp])
                V(lambda: ve.tensor_tensor(out=concat[:, t, gg * 256:(gg + 1) * 256], in0=tmp[:, 0:256],
                                           in1=concat[:, t, gg * 256:(gg + 1) * 256], op=ALU.mult),
                  r=[T_tmp, T_cc[t]], w=[T_cc[t]])

        if STAGE < 4:
            return
        if g == 1:
            sample_pass2_preamble_b()
        diff_v_loads((0, 1))

        for hp in (1, 0):
            for t in range(NT):
                qb, T_qb = rotqblk()
                for j in range(4):
                    base = j * 32
                    G(lambda: ge.tensor_copy(out=qb[base:base + 32, j * 128:(j + 1) * 128],
                                             in_=qdT[base:base + 32, hp, t * 128:(t + 1) * 128]), r=[T_qdT[t]], w=[T_qb])

                def qk_fn(sbk, kt):
                    P(lambda: te.matmul(PB[sbk], lhsT=dkT[:, hp, kt * 128:(kt + 1) * 128], rhs=qb[:, :],
                                        start=True, stop=True), r=[DKTr[rg_of(kt)], T_qb], w=[T_PB[sbk]])

                def pv_fn(ob, pbuf, T_pb, ktp, first, last):
                    for j in range(2):
                        for hh in range(2):
                            P(lambda: te.matmul(PB[ob][0:65, hh * 256:(hh + 1) * 256], lhsT=VA4[:, ktp[j], hp * 2 + hh, :],
                                                rhs=pbuf[:, j * 512 + hh * 256:j * 512 + (hh + 1) * 256],
                                                start=(first and j == 0 and hh == 0), stop=(last and j == 1),
                                                skip_group_check=True), r=[T_pb, VATr[hp][rg_of(ktp[j])]], w=[T_PB[ob]])

                ob = attn_unit(t, qk_fn, pv_fn, 32 ** -0.5)
                o4 = PB[ob][:, 0:260].rearrange("p (h m c) -> p h m c", h=2, m=2)
                rs, T_rs = rot16()
                V(lambda: ve.reciprocal(out=rs[:, 0:4].rearrange("p (h m) -> p h m", m=2), in_=o4[:, :, :, 64]),
                  r=[T_PB[ob]], w=[T_rs])
                rs3 = rs[:, 0:4].rearrange("p (h m) -> p h m", m=2)
                V(lambda: ve.tensor_scalar(out=rs3[:, :, 1], in0=rs3[:, :, 1], scalar1=lamt[:, 1:2], scalar2=None,
                                           op0=ALU.mult), r=[T_rs, T_lam], w=[T_rs])
                t1, T_t1 = rot512()
                V(lambda: ve.tensor_tensor(out=t1[:, 0:256].rearrange("p (h m d) -> p h m d", h=2, m=2), in0=o4[:, :, :, 0:64],
                                           in1=bview(rs3, [128, 2, 2, 64], 3), op=ALU.mult),
                  r=[T_PB[ob], T_rs], w=[T_t1])
                t14 = t1[:, 0:256].rearrange("p (h m d) -> p h m d", h=2, m=2)
                od, T_od = rot512()
                od3 = od[:, 0:128].rearrange("p (h d) -> p h d", h=2)
                V(lambda: ve.tensor_tensor(out=od3, in0=t14[:, :, 0, :], in1=t14[:, :, 1, :], op=ALU.add),
                  r=[T_t1], w=[T_od])
                V(lambda: ve.tensor_tensor(out=t1[:, 256:384], in0=od[:, 0:128], in1=od[:, 0:128], op=ALU.mult),
                  r=[T_od], w=[T_t1])
                rsd, T_rsd = rstd_of(t1[:, 256:384], T_t1, 2)
                V(lambda: ve.tensor_tensor(out=od3, in0=od3, in1=bview(rsd, [128, 2, 64], 2), op=ALU.mult),
                  r=[T_od, T_rsd], w=[T_od])
                V(lambda: ve.tensor_tensor(out=od3, in0=od3, in1=bview(gn[:, 128:192], [128, 2, 64], 1), op=ALU.mult),
                  r=[T_od, T_gn], w=[T_od])
                lo = 512 + hp * 128
                V(lambda: ve.tensor_tensor(out=concat[:, t, lo:lo + 128], in0=od[:, 0:128], in1=concat[:, t, lo:lo + 128],
                                           op=ALU.mult), r=[T_od, T_cc[t]], w=[T_cc[t]])

        if STAGE < 5:
            return
        if g == 0:
            for bb in range(4):
                for d in range(2):
                    V(lambda: ve.memset(Srun[:, d, :, :], 0.0), w=[T_Srun[d]])
                run_both([2 * bb, 2 * bb + 1])
                for d in range(2):
                    ns, T_ns = rotnst()
                    V(lambda: ve.tensor_copy(out=ns[:, :, :], in_=Srun[:, d, :, :]), r=[T_Srun[d]], w=[T_ns])
                    tk = Tok("o_st")
                    out_toks.append(tk)
                    SP.dma(out=st_view(nst[bb, l, d]), in_=ns[:, :, :], r=[T_ns], w=[tk], owner=T_ns)
        else:
            tiles = list(range(NT))
            run_both(tiles)

        if STAGE < 6:
            return
        wsrc_o = woutb[l].rearrange("(c p) n -> p c n", p=128)
        SP.dma(out=wout_v[:, :, :], in_=wsrc_o, r=[T_woutb[l]], w=[W_OUT])
        SP.dma(out=lnbc, in_=lnbc_d[l], w=[LNBC])
        cT_bufs = [(cT_d, T_cTd), rothT.items[0]]

        def out_gen(t):
            xt = xres[:, t, :]
            mu = mu_all[:, t, :]
            T_mu = T_mua[t]
            sq, T_sq = rot512()
            V(lambda: ve.tensor_tensor(out=sq[:, 0:256], in0=oacc[:, t, :], in1=oacc[:, t, :], op=ALU.mult),
              r=[T_oacc[t]], w=[T_sq])
            rs, T_rs = rstd_of(sq[:, 0:256], T_sq, 4)
            yield
            o3 = oacc[:, t, :].rearrange("p (h d) -> p h d", d=64)
            V(lambda: ve.tensor_tensor(out=o3, in0=o3, in1=bview(rs, [128, 4, 64], 2), op=ALU.mult),
              r=[T_oacc[t], T_rs], w=[T_oacc[t]])
            V(lambda: ve.tensor_tensor(out=o3, in0=o3, in1=bview(gn[:, 192:256], [128, 4, 64], 1), op=ALU.mult),
              r=[T_oacc[t], T_gn], w=[T_oacc[t]])
            V(lambda: ve.tensor_tensor(out=concat[:, t, 768:1024], in0=oacc[:, t, :], in1=concat[:, t, 768:1024],
                                       op=ALU.mult), r=[T_oacc[t], T_cc[t]], w=[T_cc[t]])
            yield
            pt = rotPT()
            for c in range(8):
                P(lambda: te.transpose(out=PT[pt][:, c * 128:(c + 1) * 128], in_=concat[:, t, c * 128:(c + 1) * 128],
                                       identity=ident[:, :]), r=[T_cc[t], T_id], w=[T_PT[pt]])
            cT, T_cT = cT_bufs[t % 2]
            A(lambda: se.activation(out=cT[:, :, :].rearrange("p c n -> p (c n)"), in_=PT[pt][:, :], func=AF.Copy),
              r=[T_PT[pt]], w=[T_cT])
            yield
            tmp, T_tmp = rot1024()
            for j in range(2):
                pb = rotPB()
                for c in range(8):
                    P(lambda: te.matmul(PB[pb], lhsT=cT[:, c, :], rhs=wout_v[:, c, j * 512:(j + 1) * 512],
                                        start=(c == 0), stop=(c == 7)), r=[T_cT, W_OUT], w=[T_PB[pb]])
                V(lambda: ve.tensor_tensor(out=tmp[:, j * 512:(j + 1) * 512], in0=PB[pb],
                                           in1=modbc[:, j * 512:(j + 1) * 512], op=ALU.mult),
                  r=[T_PB[pb], T_mod], w=[T_tmp])
            V(lambda: ve.scalar_tensor_tensor(out=xt, in0=xt, scalar=ALPHA, in1=tmp[:, :], op0=ALU.mult,
                                              op1=ALU.add), r=[T_x[t], T_tmp], w=[T_x[t]])
            V(lambda: ve.tensor_reduce(out=mu[:, 0:1], in_=xt, axis=AX.X, op=ALU.add), r=[T_x[t]], w=[T_mu])
            V(lambda: ve.tensor_scalar(out=mu[:, 0:1], in0=mu[:, 0:1], scalar1=-1.0 / 1024, scalar2=None, op0=ALU.mult),
              r=[T_mu], w=[T_mu])
            yield
            V(lambda: ve.tensor_scalar(out=xt, in0=xt, scalar1=mu[:, 0:1], scalar2=None, op0=ALU.add),
              r=[T_x[t], T_mu], w=[T_x[t]])
            V(lambda: ve.memset(mu[:, 1:2], 0.0), r=[T_mu], w=[T_mu])
            junk = concat[:, t, :]
            A(lambda: se.activation(out=junk, in_=xt, func=AF.Square, accum_out=mu[:, 1:2]),
              r=[T_x[t], T_mu], w=[T_cc[t], T_mu])
            yield
            V(lambda: ve.tensor_scalar(out=mu[:, 1:2], in0=mu[:, 1:2], scalar1=1.0 / 1024, scalar2=LN_EPS, op0=ALU.mult,
                                       op1=ALU.add), r=[T_mu], w=[T_mu])
            A(lambda: se.activation(out=mu[:, 1:2], in_=mu[:, 1:2], func=AF.Ln), r=[T_mu], w=[T_mu])
            A(lambda: se.activation(out=mu[:, 1:2], in_=mu[:, 1:2], func=AF.Exp, scale=-0.5), r=[T_mu], w=[T_mu])
            yield
            V(lambda: ve.tensor_scalar(out=xt, in0=xt, scalar1=mu[:, 1:2], scalar2=None, op0=ALU.mult),
              r=[T_x[t], T_mu], w=[T_x[t]])
            G(lambda: ge.tensor_tensor(out=xt, in0=xt, in1=lnbc[:, 0:1024], op=ALU.mult),
              r=[T_x[t], LNBC], w=[T_x[t]])
            yield
            V(lambda: ve.tensor_tensor(out=xt, in0=xt, in1=lnbc[:, 1024:2048], op=ALU.add),
              r=[T_x[t], LNBC], w=[T_x[t]])
            if l == L - 1:
                tk = Tok("o_y")
                out_toks.append(tk)
                SP.dma(out=y_out[g][t * 128:(t + 1) * 128, :], in_=xt, r=[T_x[t]], w=[tk], owner=T_x[t])

        NSTEP = 8
        ogens = [out_gen(t) for t in range(NT)]
        for i in range(NT + NSTEP - 1):
            for k_ in reversed(range(NSTEP)):
                j = i - k_
                if 0 <= j < NT:
                    next(ogens[j], None)

    ph = 0
    for g in range(2):
        for l in range(L):
            if ph < NPH:
                phase(g, l, ph)
            ph += 1

    evs = []
    for tk in out_toks + list(dbg_outs.values()):
        evs += list(tk.w.items())
    SP._wait(evs)
    es.close()
    return nc


def _consts(core):
    cf = np.zeros((128, 656), np.float32)
    cf[:, 528:656] = np.eye(128, dtype=np.float32)
    s = np.arange(128)[:, None]
    t = np.arange(128)[None, :]
    same = (s // 64) == (t // 64)
    cs = (t // 64) * 64
    midf = cs + 31
    midb = cs + 32
    cf[:, 0:128] = same * ((s <= t).astype(np.float32) - (s <= midf).astype(np.float32))
    cf[:, 128:256] = same * ((s >= t).astype(np.float32) - (s >= midb).astype(np.float32))
    cf[:, 256:384] = same * (s <= t)
    cf[:, 384:512] = same * (s >= t)
    sv = np.arange(128)
    cf[:, 512] = sv < 64
    cf[:, 513] = sv >= 64
    cf[:, 514] = sv <= 31
    cf[:, 515] = (sv >= 64) & (sv <= 95)
    cf[:, 516] = sv < 64
    cf[:, 517] = sv >= 64
    cf[:, 518] = (sv >= 32) & (sv < 64)
    cf[:, 519] = sv >= 96
    cf[:, 520 + (core % 4)] = 1.0
    return cf


def _rope_tab(pos0):
    pos = pos0 + np.arange(1024)
    row = (pos // 64).astype(np.float32)
    col = (pos % 64).astype(np.float32)
    out = np.zeros((1024, 192), np.float32)

    def tab(dim):
        q = dim // 4
        inv = (np.float32(10000.0) ** (-np.arange(q, dtype=np.float32) / np.float32(q))).astype(np.float32)
        ar = row[:, None] * inv[None, :]
        ac = col[:, None] * inv[None, :]
        ang = np.concatenate([ar, ar, ac, ac], axis=-1).astype(np.float32)
        cos, sin = np.cos(ang), np.sin(ang)
        sgn = np.concatenate([-np.ones(q), np.ones(q), -np.ones(q), np.ones(q)]).astype(np.float32)
        return cos.astype(np.float32), (sin * sgn[None, :]).astype(np.float32)

    c, s_ = tab(64)
    out[:, 0:64], out[:, 64:128] = c, s_
    c, s_ = tab(32)
    out[:, 128:160], out[:, 160:192] = c, s_
    return out


_NC_CACHE = {}


def kernel(x_prompt, x_sample, cache_gqa_k, cache_gqa_v, cache_diff_k, cache_diff_v, state_hgrn,
           c, c_ctx, w_ada, b_ada, w_in, gqa_q_norm, gqa_k_norm, diff_lambda, diff_subln,
           hgrn_lower_bounds, hgrn_norm, w_out, ln_g, ln_b, _debug=None):
    f = lambda a: np.ascontiguousarray(np.asarray(a, dtype=np.float32))
    x_prompt, x_sample = f(x_prompt), f(x_sample)
    key = tuple(sorted(_debug)) if _debug else None
    if key not in _NC_CACHE:
        _NC_CACHE[key] = build(debug=_debug)
    nc = _NC_CACHE[key]
    rep = lambda v: np.ascontiguousarray(np.broadcast_to(np.asarray(v, np.float32).reshape(1, -1), (128, np.asarray(v).size)))
    gains = np.stack([np.concatenate([rep(gqa_q_norm[l]), rep(gqa_k_norm[l]), rep(diff_subln[l]), rep(hgrn_norm[l])], axis=1)
                      for l in range(L)])
    dlam = np.stack([rep(np.asarray(diff_lambda[l]).reshape(-1)) for l in range(L)])
    hlb = rep(np.asarray(hgrn_lower_bounds).reshape(-1))
    lnbc = np.stack([np.concatenate([rep(ln_g[l]), rep(ln_b[l])], axis=1) for l in range(L)])
    ident = np.eye(128, dtype=np.float32)
    ba = f(b_ada)
    badaT = np.ascontiguousarray(ba[:, 0:2048].reshape(L, 16, 128).transpose(0, 2, 1))
    badag = np.stack([rep(ba[l, 2048:3072]) for l in range(L)])
    shared = {"w_ada": f(w_ada), "badaT": badaT, "badag": f(badag), "w_in": f(w_in), "w_out": f(w_out),
              "gains": f(gains), "dlam": f(dlam), "hlb": f(hlb), "lnbc": f(lnbc), "ident": ident}
    in_maps = []
    for core in range(8):
        b = core // 4
        p0 = (core % 4) * 1024
        cond = np.stack([np.asarray(c_ctx, np.float32), np.asarray(c, np.float32)[b]])
        condT = np.ascontiguousarray(cond.reshape(2, 8, 128).transpose(0, 2, 1))
        m = dict(shared)
        m.update({
            "xp": f(x_prompt[core * 4:(core + 1) * 4].reshape(1024, 1024)),
            "xs": f(x_sample[b, p0:p0 + 1024]),
            "condT": condT,
            "cgk": f(np.asarray(cache_gqa_k)[b].reshape(L, 256, 128)),
            "cgv": f(np.asarray(cache_gqa_v)[b].reshape(L, 256, 128)),
            "cdk": f(np.asarray(cache_diff_k)[b].reshape(L, 256, 256)),
            "cdv": f(np.asarray(cache_diff_v)[b].reshape(L, 256, 256)),
            "st0": f(np.asarray(state_hgrn)[b]),
            "rope": _rope_tab(p0),
            "cf": _consts(core),
        })
        in_maps.append(m)
    res = run_bass_kernel_spmd(nc, in_maps, core_ids=list(range(8)))
    R = res.results
    y_prompt = np.concatenate([R[i]["yp"].reshape(4, 256, 1024) for i in range(8)], axis=0)
    y_sample = np.stack([np.concatenate([R[b * 4 + j]["ys"] for j in range(4)], axis=0) for b in range(2)])
    ngk = np.concatenate([R[i]["ngk"].reshape(4, L, 256, 2, 64) for i in range(8)], axis=0)
    ngv = np.concatenate([R[i]["ngv"].reshape(4, L, 256, 2, 64) for i in range(8)], axis=0)
    ndk = np.concatenate([R[i]["ndk"].reshape(4, L, 256, 4, 64) for i in range(8)], axis=0)
    ndv = np.concatenate([R[i]["ndv"].reshape(4, L, 256, 4, 64) for i in range(8)], axis=0)
    nst = np.concatenate([R[i]["nst"] for i in range(8)], axis=0)
    outs = (y_prompt.astype(np.float32), y_sample.astype(np.float32), ngk.astype(np.float32), ngv.astype(np.float32),
            ndk.astype(np.float32), ndv.astype(np.float32), nst.astype(np.float32))
    if _debug:
        return outs, [{k: v for k, v in R[i].items() if k.startswith("dbg_")} for i in range(8)]
    return outs
```
